# Optimizing a Trainium2 kernel written in Bass

```python
import math
import jax
import jax.numpy as jnp
from jax import lax
import numpy as np

D_MODEL = 1024
BATCH = 2
SEQ = 8192
DEPTH = 2

N_META = 16
Q_BLOCK = 128
LN_EPS = 1e-5
RMS_EPS = 1e-6
NEG_INF = -1e30
DEEPNORM_ALPHA = (2.0 * DEPTH) ** 0.25
DEEPNORM_BETA = (8.0 * DEPTH) ** -0.25

LRU_WIDTH = 256
LRU_BLOCKS = 4
LRU_BLOCK_DIM = LRU_WIDTH // LRU_BLOCKS
LRU_CONV = 4
LRU_C = 8.0
FOX_HEADS = 4
FOX_HEAD_DIM = 64
FOX_WIDTH = FOX_HEADS * FOX_HEAD_DIM
MLA_HEADS = 4
MLA_Q_RANK = 256
MLA_KV_RANK = 128
MLA_NOPE_DIM = 64
MLA_ROPE_DIM = 32
MLA_V_DIM = 64
MLA_WIDTH = MLA_HEADS * MLA_V_DIM
ROPE_BASE = 10000.0
RWKV_HEADS = 4
RWKV_HEAD_DIM = 64
RWKV_WIDTH = RWKV_HEADS * RWKV_HEAD_DIM
RWKV_DECAY_RANK = 32
RWKV_AAA_RANK = 32
RWKV_GATE_RANK = 64
RWKV_GN_EPS = 64e-5
RWKV_SPLITS = (RWKV_WIDTH, RWKV_WIDTH, RWKV_WIDTH, RWKV_DECAY_RANK, RWKV_AAA_RANK, RWKV_GATE_RANK)
RWKV_IN_WIDTH = sum(RWKV_SPLITS)
N_BRANCH = 4
BRANCH_WIDTH = 256
D_FF = 2816
FFN_CONV = 3

IN_SPLITS = (LRU_WIDTH, LRU_WIDTH, FOX_WIDTH, FOX_WIDTH, FOX_WIDTH, FOX_HEADS,
             MLA_Q_RANK, MLA_KV_RANK, MLA_ROPE_DIM, RWKV_IN_WIDTH, N_BRANCH * D_MODEL)
D_IN = sum(IN_SPLITS)

kernel_name = 'hybrid_lru_fox_mla_rwkv7_block'


def split_last(z, sizes):
    return jnp.split(z, [int(s) for s in np.cumsum(sizes)[:-1]], axis=-1)


def layer_norm(x, g, b):
    xf = x.astype(jnp.float32)
    mu = jnp.mean(xf, -1, keepdims=True)
    var = jnp.mean(jnp.square(xf - mu), -1, keepdims=True)
    return ((xf - mu) * lax.rsqrt(var + LN_EPS) * g + b).astype(x.dtype)


def rms_norm(x, g):
    xf = x.astype(jnp.float32)
    return (xf * lax.rsqrt(jnp.mean(jnp.square(xf), -1, keepdims=True) + RMS_EPS) * g).astype(x.dtype)


def causal_depthwise_conv(x, w, b):
    K, C = w.shape
    y = lax.conv_general_dilated(x, w[:, None, :].astype(x.dtype), window_strides=(1,),
                                 padding=[(K - 1, 0)], dimension_numbers=('NWC', 'WIO', 'NWC'),
                                 feature_group_count=C)
    return y + b


def token_shift(z):
    return jnp.pad(z[:, :-1], ((0, 0), (1, 0), (0, 0)))


def rotary_tables(T, dim):
    inv = 1.0 / (ROPE_BASE ** (jnp.arange(0, dim, 2, dtype=jnp.float32) / dim))
    ang = jnp.arange(T, dtype=jnp.float32)[:, None] * inv[None, :]
    return jnp.cos(ang), jnp.sin(ang)


def apply_rotary(x, cos, sin):
    x1, x2 = jnp.split(x.astype(jnp.float32), 2, axis=-1)
    return jnp.concatenate([x1 * cos - x2 * sin, x2 * cos + x1 * sin], axis=-1).astype(x.dtype)


def block_causal_attention(q, k, v, cum_log_f=None):
    B, T, H, dk = q.shape
    pad_front = (-N_META) % Q_BLOCK
    pad_back = (-(pad_front + T)) % Q_BLOCK
    L = pad_front + T + pad_back
    nb = L // Q_BLOCK
    pad_t = lambda z: jnp.pad(z, ((0, 0), (pad_front, pad_back)) + ((0, 0),) * (z.ndim - 2))
    q, k, v = pad_t(q), pad_t(k), pad_t(v)
    kpos = jnp.arange(L)
    kvalid = kpos >= pad_front
    scale = 1.0 / math.sqrt(dk)
    xs = [jnp.arange(nb), jnp.moveaxis(q.reshape(B, nb, Q_BLOCK, H, dk), 1, 0)]
    F_bhl = None
    if cum_log_f is not None:
        F = pad_t(cum_log_f.astype(jnp.float32))
        F_bhl = jnp.swapaxes(F, 1, 2)
        xs.append(jnp.moveaxis(F.reshape(B, nb, Q_BLOCK, H), 1, 0))

    def one_block(args):
        i, q_i = args[0], args[1]
        qpos = i * Q_BLOCK + jnp.arange(Q_BLOCK)
        s = jnp.einsum('bqhd,bkhd->bhqk', q_i, k, preferred_element_type=jnp.float32) * scale
        if F_bhl is not None:
            f_i = jnp.swapaxes(args[2], 1, 2)
            s = s + (f_i[..., :, None] - F_bhl[:, :, None, :])
        mask = (kpos[None, :] <= qpos[:, None]) & kvalid[None, :]
        s = jnp.where(mask, s, NEG_INF)
        p = jax.nn.softmax(s, axis=-1).astype(v.dtype)
        return jnp.einsum('bhqk,bkhd->bqhd', p, v)

    out = lax.map(one_block, tuple(xs))
    out = jnp.moveaxis(out, 0, 1).reshape(B, L, H, v.shape[-1])
    return out[:, pad_front:pad_front + T]


def rglru_branch(y_in, x_in, conv_w, conv_b, w_rg, b_rg, w_ig, b_ig, lam):
    B, T, _ = x_in.shape
    f32 = jnp.float32
    xc = causal_depthwise_conv(x_in, conv_w, conv_b)
    xb = xc.reshape(B, T, LRU_BLOCKS, LRU_BLOCK_DIM)
    r = jax.nn.sigmoid((jnp.einsum('btnd,nde->btne', xb, w_rg).reshape(B, T, LRU_WIDTH) + b_rg).astype(f32))
    i = jax.nn.sigmoid((jnp.einsum('btnd,nde->btne', xb, w_ig).reshape(B, T, LRU_WIDTH) + b_ig).astype(f32))
    log_a = -LRU_C * r * jax.nn.softplus(-lam.astype(f32))
    a = jnp.exp(log_a)
    u = jnp.sqrt(-jnp.expm1(2.0 * log_a)) * (i * xc.astype(f32))

    def combine(left, right):
        a_l, h_l = left
        a_r, h_r = right
        return a_l * a_r, a_r * h_l + h_r

    _, h = lax.associative_scan(combine, (a, u), axis=1)
    return (jax.nn.gelu(y_in.astype(f32)) * h).astype(x_in.dtype)


def fox_branch(q, k, v, f_logit, b_f):
    B, T, _ = q.shape
    heads = lambda z: z.reshape(B, T, FOX_HEADS, FOX_HEAD_DIM)
    log_f = jax.nn.log_sigmoid(f_logit.astype(jnp.float32) + b_f)
    F = jnp.cumsum(log_f, axis=1)
    o = block_causal_attention(heads(q), heads(k), heads(v), F)
    return o.reshape(B, T, FOX_WIDTH)


def mla_branch(c_q, c_kv, k_r, q_norm_g, kv_norm_g, w_uq, w_ukv, cos, sin):
    B, T, _ = c_q.shape
    q = (rms_norm(c_q, q_norm_g) @ w_uq).reshape(B, T, MLA_HEADS, MLA_NOPE_DIM + MLA_ROPE_DIM)
    q_nope, q_rope = jnp.split(q, [MLA_NOPE_DIM], axis=-1)
    q_rope = apply_rotary(q_rope, cos[:, None, :], sin[:, None, :])
    kv = (rms_norm(c_kv, kv_norm_g) @ w_ukv).reshape(B, T, MLA_HEADS, MLA_NOPE_DIM + MLA_V_DIM)
    k_nope, v = jnp.split(kv, [MLA_NOPE_DIM], axis=-1)
    k_rope = apply_rotary(k_r, cos, sin)
    k = jnp.concatenate([k_nope, jnp.broadcast_to(k_rope[:, :, None, :], (B, T, MLA_HEADS, MLA_ROPE_DIM))], axis=-1)
    qf = jnp.concatenate([q_nope, q_rope], axis=-1)
    o = block_causal_attention(qf, k, v)
    return o.reshape(B, T, MLA_WIDTH)


def rwkv7_scan(r, w, k, v, a, b):
    B, T, H, N = r.shape

    def step(S, inp):
        r_t, w_t, k_t, v_t, a_t, b_t = inp
        sa = jnp.einsum('bhij,bhj->bhi', S, a_t)
        S = S * w_t[:, :, None, :] + sa[..., None] * b_t[:, :, None, :] + v_t[..., None] * k_t[:, :, None, :]
        return S, jnp.einsum('bhij,bhj->bhi', S, r_t)

    xs = tuple(jnp.moveaxis(t, 1, 0) for t in (r, w, k, v, a, b))
    _, ys = lax.scan(step, jnp.zeros((B, H, N, N), jnp.float32), xs)
    return jnp.moveaxis(ys, 0, 1)


def rwkv7_branch(p_in, mu, w0, w2, a0, a2, g2, k_k, k_a, r_k, gn_g, gn_b):
    B, T, _ = p_in.shape
    f32 = jnp.float32
    p = p_in + (token_shift(p_in) - p_in) * mu
    r, k, v, pw, pa, pg = split_last(p, RWKV_SPLITS)
    z = (w0 + jnp.tanh(pw) @ w2).astype(f32)
    decay = jnp.exp(-math.exp(-0.5) * jax.nn.sigmoid(z))
    a = jax.nn.sigmoid((a0 + pa @ a2).astype(f32))
    g = jax.nn.sigmoid(pg) @ g2
    heads = lambda t: t.reshape(B, T, RWKV_HEADS, RWKV_HEAD_DIM)
    kk = heads(k.astype(f32) * k_k)
    kk = kk * lax.rsqrt(jnp.maximum(jnp.sum(jnp.square(kk), -1, keepdims=True), 1e-24))
    kf = heads(k.astype(f32) * (1.0 + (a - 1.0) * k_a))
    rf, vf, af = heads(r.astype(f32)), heads(v.astype(f32)), heads(a)
    y = rwkv7_scan(rf, heads(decay), kf, vf, -kk, kk * af)
    y_mu = jnp.mean(y, -1, keepdims=True)
    y_var = jnp.mean(jnp.square(y - y_mu), -1, keepdims=True)
    y = (y - y_mu) * lax.rsqrt(y_var + RWKV_GN_EPS) * gn_g.reshape(RWKV_HEADS, RWKV_HEAD_DIM) \
        + gn_b.reshape(RWKV_HEADS, RWKV_HEAD_DIM)
    y = y + jnp.sum(rf * kf * r_k, -1, keepdims=True) * vf
    return (y.reshape(B, T, RWKV_WIDTH) * g).astype(p_in.dtype)


def conv_glu_ffn(x, w_up, conv_w, conv_b, w_down):
    hdn = causal_depthwise_conv(x @ w_up, conv_w, conv_b)
    gate, val = jnp.split(hdn, 2, axis=-1)
    return (jax.nn.silu(gate) * val) @ w_down


def setup_inputs(seed: int = 0) -> dict:
    key = jax.random.key(seed)
    ks = list(jax.random.split(key, 48))
    f32 = jnp.float32
    L = DEPTH

    def nrm(shape, scale):
        return jax.random.normal(ks.pop(), shape, f32) * scale

    def gain(shape):
        return 1.0 + nrm(shape, 0.02)

    u = jax.random.uniform(ks.pop(), (L, LRU_WIDTH), f32, 0.9, 0.999)
    a_base = u ** (1.0 / LRU_C)
    lru_lambda = jnp.log(a_base) - jnp.log1p(-a_base)
    rwkv_mu = jax.random.uniform(ks.pop(), (L, RWKV_IN_WIDTH), f32)
    return {
        'x': nrm((BATCH, SEQ, D_MODEL), 1.0),
        'meta_tokens': nrm((N_META, D_MODEL), 1.0),
        'ln_in_g': gain((D_MODEL,)),
        'ln_in_b': nrm((D_MODEL,), 0.02),
        'w_in': nrm((L, D_MODEL, D_IN), D_MODEL ** -0.5),
        'w_branch': nrm((L, N_BRANCH, BRANCH_WIDTH, D_MODEL), BRANCH_WIDTH ** -0.5),
        'w_out': nrm((L, D_MODEL, D_MODEL), DEEPNORM_BETA * D_MODEL ** -0.5),
        'ln_mix_g': gain((L, D_MODEL)),
        'ln_mix_b': nrm((L, D_MODEL), 0.02),
        'lru_conv_w': nrm((L, LRU_CONV, LRU_WIDTH), LRU_CONV ** -0.5),
        'lru_conv_b': nrm((L, LRU_WIDTH), 0.02),
        'lru_w_rg': nrm((L, LRU_BLOCKS, LRU_BLOCK_DIM, LRU_BLOCK_DIM), LRU_BLOCK_DIM ** -0.5),
        'lru_b_rg': nrm((L, LRU_WIDTH), 0.02),
        'lru_w_ig': nrm((L, LRU_BLOCKS, LRU_BLOCK_DIM, LRU_BLOCK_DIM), LRU_BLOCK_DIM ** -0.5),
        'lru_b_ig': nrm((L, LRU_WIDTH), 0.02),
        'lru_lambda': lru_lambda,
        'fox_b_f': nrm((L, FOX_HEADS), 0.02),
        'mla_q_norm_g': gain((L, MLA_Q_RANK)),
        'mla_kv_norm_g': gain((L, MLA_KV_RANK)),
        'mla_w_uq': nrm((L, MLA_Q_RANK, MLA_HEADS * (MLA_NOPE_DIM + MLA_ROPE_DIM)), MLA_Q_RANK ** -0.5),
        'mla_w_ukv': nrm((L, MLA_KV_RANK, MLA_HEADS * (MLA_NOPE_DIM + MLA_V_DIM)), MLA_KV_RANK ** -0.5),
        'rwkv_mu': rwkv_mu,
        'rwkv_w0': nrm((L, RWKV_WIDTH), 0.5),
        'rwkv_w2': nrm((L, RWKV_DECAY_RANK, RWKV_WIDTH), 0.1 * RWKV_DECAY_RANK ** -0.5),
        'rwkv_a0': nrm((L, RWKV_WIDTH), 0.1),
        'rwkv_a2': nrm((L, RWKV_AAA_RANK, RWKV_WIDTH), 0.1 * RWKV_AAA_RANK ** -0.5),
        'rwkv_g2': nrm((L, RWKV_GATE_RANK, RWKV_WIDTH), RWKV_GATE_RANK ** -0.5),
        'rwkv_k_k': 0.85 + nrm((L, RWKV_WIDTH), 0.02),
        'rwkv_k_a': gain((L, RWKV_WIDTH)),
        'rwkv_r_k': nrm((L, RWKV_HEADS, RWKV_HEAD_DIM), 0.1),
        'rwkv_gn_g': gain((L, RWKV_WIDTH)),
        'rwkv_gn_b': nrm((L, RWKV_WIDTH), 0.02),
        'ffn_w_up': nrm((L, D_MODEL, 2 * D_FF), D_MODEL ** -0.5),
        'ffn_conv_w': nrm((L, FFN_CONV, 2 * D_FF), FFN_CONV ** -0.5),
        'ffn_conv_b': nrm((L, 2 * D_FF), 0.02),
        'ffn_w_down': nrm((L, D_FF, D_MODEL), DEEPNORM_BETA * D_FF ** -0.5),
        'ln_ffn_g': gain((L, D_MODEL)),
        'ln_ffn_b': nrm((L, D_MODEL), 0.02),
    }


def reference(x, meta_tokens, ln_in_g, ln_in_b, w_in, w_branch, w_out, ln_mix_g, ln_mix_b,
              lru_conv_w, lru_conv_b, lru_w_rg, lru_b_rg, lru_w_ig, lru_b_ig, lru_lambda,
              fox_b_f, mla_q_norm_g, mla_kv_norm_g, mla_w_uq, mla_w_ukv,
              rwkv_mu, rwkv_w0, rwkv_w2, rwkv_a0, rwkv_a2, rwkv_g2, rwkv_k_k, rwkv_k_a, rwkv_r_k,
              rwkv_gn_g, rwkv_gn_b, ffn_w_up, ffn_conv_w, ffn_conv_b, ffn_w_down, ln_ffn_g, ln_ffn_b):
    B = x.shape[0]
    meta = jnp.broadcast_to(meta_tokens.astype(x.dtype)[None], (B, N_META, D_MODEL))
    h = layer_norm(jnp.concatenate([meta, x], axis=1), ln_in_g, ln_in_b)
    T = h.shape[1]
    cos, sin = rotary_tables(T, MLA_ROPE_DIM)
    for l in range(DEPTH):
        (lru_y, lru_x, fox_q, fox_k, fox_v, fox_f, mla_cq, mla_ckv, mla_kr,
         rwkv_in, gate_logits) = split_last(h @ w_in[l], IN_SPLITS)
        o_a = rglru_branch(lru_y, lru_x, lru_conv_w[l], lru_conv_b[l], lru_w_rg[l], lru_b_rg[l],
                           lru_w_ig[l], lru_b_ig[l], lru_lambda[l])
        o_b = fox_branch(fox_q, fox_k, fox_v, fox_f, fox_b_f[l])
        o_c = mla_branch(mla_cq, mla_ckv, mla_kr, mla_q_norm_g[l], mla_kv_norm_g[l],
                         mla_w_uq[l], mla_w_ukv[l], cos, sin)
        o_d = rwkv7_branch(rwkv_in, rwkv_mu[l], rwkv_w0[l], rwkv_w2[l], rwkv_a0[l], rwkv_a2[l],
                           rwkv_g2[l], rwkv_k_k[l], rwkv_k_a[l], rwkv_r_k[l], rwkv_gn_g[l], rwkv_gn_b[l])
        branches = jnp.stack([o_a, o_b, o_c, o_d], axis=2)
        proj = jnp.einsum('btnc,ncd->btnd', branches, w_branch[l])
        gates = jax.nn.sigmoid(gate_logits.reshape(B, T, N_BRANCH, D_MODEL))
        mixed = jnp.sum(gates * proj, axis=2) @ w_out[l]
        h = layer_norm(DEEPNORM_ALPHA * h + mixed, ln_mix_g[l], ln_mix_b[l])
        ffn = conv_glu_ffn(h, ffn_w_up[l], ffn_conv_w[l], ffn_conv_b[l], ffn_w_down[l])
        h = layer_norm(DEEPNORM_ALPHA * h + ffn, ln_ffn_g[l], ln_ffn_b[l])
    return h[:, N_META:]
```

```python
import contextlib
import math
import numpy as np
import concourse.bass as bass
import concourse.mybir as mybir
from concourse.alu_op_type import AluOpType as ALU
from concourse.bass_utils import run_bass_kernel_spmd

F32 = mybir.dt.float32
BF16 = mybir.dt.bfloat16
AF = mybir.ActivationFunctionType
AX = mybir.AxisListType

COMPUTE = ("tensor", "vector", "scalar", "gpsimd")
QUEUES = ("sync", "scalar", "gpsimd")
NDMA_SEM = 6


class Buf:
    def __init__(self, t, name, psum=False):
        self.t = t
        self.name = name
        self.st = {}
        self.is_psum = psum

    def __getitem__(self, idx):
        return self.t[idx]

    def ap(self):
        return self.t[:]


class _St:
    __slots__ = ("writer", "readers")

    def __init__(self):
        self.writer = None
        self.readers = {}


class Prog:
    def __init__(self):
        self.nc = bass.Bass("TRN2", target_bir_lowering=False)
        self.es = contextlib.ExitStack()
        self.ops = {e: [] for e in ("tensor", "vector", "scalar", "gpsimd", "sync")}
        self.sem = {}
        self.cnt = {}
        self.semname = {}
        for e in COMPUTE:
            s = self.es.enter_context(self.nc.semaphore("c_" + e))
            self.sem[e] = s
            self.cnt[e] = 0
        self.dsem = {}
        for q in QUEUES:
            lst = []
            for i in range(NDMA_SEM):
                s = self.es.enter_context(self.nc.semaphore(f"d_{q}{i}"))
                lst.append([s, 0])
            self.dsem[q] = [lst, 0]
        self.waited = {e: {} for e in self.ops}
        self.uid = 0
        self.n_inst = 0
        self.phase_stack = None

    def dram_in(self, name, shape, dt):
        return Buf(self.nc.dram_tensor(name, list(shape), dt, kind="ExternalInput"), name)

    def dram_out(self, name, shape, dt):
        return Buf(self.nc.dram_tensor(name, list(shape), dt, kind="ExternalOutput"), name)

    def dram_tmp(self, name, shape, dt):
        return Buf(self.nc.dram_tensor(name, list(shape), dt, kind="Internal"), name)

    def _stack(self):
        return self.phase_stack if self.phase_stack is not None else self.es

    def sbuf(self, name, shape, dt):
        self.uid += 1
        t = self._stack().enter_context(self.nc.sbuf_tensor(f"{name}_{self.uid}", list(shape), dt))
        return Buf(t, name)

    def psum(self, name, shape, dt=F32):
        self.uid += 1
        t = self._stack().enter_context(self.nc.psum_tensor(f"{name}_{self.uid}", list(shape), dt))
        return Buf(t, name, psum=True)

    @contextlib.contextmanager
    def phase(self):
        assert self.phase_stack is None
        self.barrier()
        with contextlib.ExitStack() as st:
            self.phase_stack = st
            yield
            self.barrier()
            self.phase_stack = None

    @staticmethod
    def _key(d):
        if isinstance(d, tuple):
            return d[0], d[1]
        return d, None

    def _collect(self, reads, writes):
        need = {}

        def add(m):
            if m is None:
                return
            s, v = m
            if need.get(id(s), (None, -1))[1] < v:
                need[id(s)] = (s, v)

        for d in reads:
            b, k = self._key(d)
            keys = list(b.st.keys()) if k is None else [k, None]
            for kk in keys:
                st = b.st.get(kk)
                if st is not None:
                    add(st.writer)
        for d in writes:
            b, k = self._key(d)
            keys = list(b.st.keys()) if k is None else [k, None]
            for kk in keys:
                st = b.st.get(kk)
                if st is not None:
                    add(st.writer)
                    for m in st.readers.values():
                        add(m)
        return need

    def _update(self, reads, writes, marker):
        s, v = marker
        for d in reads:
            b, k = self._key(d)
            st = b.st.setdefault(k, _St())
            st.readers[id(s)] = (s, v)
        for d in writes:
            b, k = self._key(d)
            if k is None:
                b.st = {}
            st = b.st.setdefault(k, _St())
            st.writer = (s, v)
            st.readers = {}

    def _emit_waits(self, eng, need):
        w = self.waited[eng]
        for sid, (s, v) in need.items():
            if eng == "tensor" and s is self.sem.get("tensor"):
                continue
            if w.get(sid, 0) >= v:
                continue
            w[sid] = v
            self.ops[eng].append(("w", s, v))

    def _psum_excl(self, reads, writes):
        r2, w2 = [], list(writes)
        for d in reads:
            b, _ = self._key(d)
            (w2 if b.is_psum else r2).append(d)
        return r2, w2

    def op(self, eng, name, *args, reads=(), writes=(), **kw):
        reads, writes = self._psum_excl(reads, writes)
        need = self._collect(reads, writes)
        self._emit_waits(eng, need)
        self.cnt[eng] += 1
        s = self.sem[eng]
        self.ops[eng].append(("i", name, args, kw, s, 1))
        self._update(reads, writes, (s, self.cnt[eng]))
        self.n_inst += 1

    def pe(self, name, *a, **k):
        self.op("tensor", name, *a, **k)

    def dve(self, name, *a, **k):
        self.op("vector", name, *a, **k)

    def act(self, name, *a, **k):
        self.op("scalar", name, *a, **k)

    def pool(self, name, *a, **k):
        self.op("gpsimd", name, *a, **k)

    def dma(self, q, out, in_, reads=(), writes=(), **kw):
        need = self._collect(reads, writes)
        lst, idx = self.dsem[q]
        ent = lst[idx % NDMA_SEM]
        self.dsem[q][1] = idx + 1
        s = ent[0]
        if ent[1] > 0:
            need[id(s)] = (s, ent[1])
        self._emit_waits(q, need)
        ent[1] += 16
        kw = dict(kw); kw["out"] = out; kw["in_"] = in_
        self.ops[q].append(("i", "dma_start", (), kw, s, 16))
        self._update(reads, writes, (s, ent[1]))
        self.n_inst += 1

    def barrier(self):
        need = {}
        for e in COMPUTE:
            if self.cnt[e] > 0:
                need[id(self.sem[e])] = (self.sem[e], self.cnt[e])
        for q in QUEUES:
            for s, v in self.dsem[q][0]:
                if v > 0:
                    need[id(s)] = (s, v)
        for e in self.ops:
            self._emit_waits(e, dict(need))

    def finish(self):
        self.barrier()
        nc = self.nc
        ops = self.ops
        with nc.Block() as block:
            def run(e, lst):
                for o in lst:
                    if o[0] == "w":
                        e.wait_ge(o[1], o[2])
                    else:
                        getattr(e, o[1])(*o[2], **o[3]).then_inc(o[4], o[5])

            @block.tensor
            def _(e):
                run(e, ops["tensor"])

            @block.vector
            def _(e):
                run(e, ops["vector"])

            @block.scalar
            def _(e):
                run(e, ops["scalar"])

            @block.gpsimd
            def _(e):
                run(e, ops["gpsimd"])

            @block.sync
            def _(e):
                run(e, ops["sync"])
        self.es.close()
        return nc


D = 1024
KC = 8
DFF = 2816
NFC = 22
LN_EPS = 1e-5
ALPHA = (2.0 * 2) ** 0.25

V_LNMIX_G, V_LNMIX_B, V_LNFFN_G, V_LNFFN_B = 0, 8, 16, 24
V_CW = 32
V_CB = 32 + 132
V_FLAG = V_CB + 44
NV = V_FLAG + 1


def tiles_of(total, width, first=None):
    out = []
    c = 0
    if first is not None:
        out.append((0, first))
        c = first
    while c < total:
        n = min(width, total - c)
        out.append((c, n))
        c += n
    return out


class LNBufs:
    def __init__(self, p, nmax):
        self.zsq = p.sbuf("zsq", [128, KC, nmax], F32)
        self.m = p.sbuf("ln_m", [128, nmax], F32)
        self.msq = p.sbuf("ln_msq", [128, nmax], F32)
        self.r = p.sbuf("ln_r", [128, nmax], F32)
        self.mr = p.sbuf("ln_mr", [128, nmax], F32)
        self.t = [p.sbuf("ln_t%d" % i, [128, nmax], F32) for i in range(2)]
        self.t2 = [p.sbuf("ln_u%d" % i, [128, nmax], F32) for i in range(2)]


def ln_tile(p, z, n, vecs, gcol, bcol, outf, outb, lb, ones_f, ps1, ps2):
    p.act("activation", out=lb.zsq[:, :, :n], in_=z[:, :, :n], func=AF.Square, reads=[z], writes=[lb.zsq])
    for kc in range(KC):
        p.pe("matmul", ps1[:, :n], ones_f[:], z[:, kc, :n], start=(kc == 0), stop=(kc == KC - 1),
             reads=[z, ones_f], writes=[ps1])
    for kc in range(KC):
        p.pe("matmul", ps2[:, :n], ones_f[:], lb.zsq[:, kc, :n], start=(kc == 0), stop=(kc == KC - 1),
             reads=[lb.zsq, ones_f], writes=[ps2])
    p.act("activation", out=lb.m[:, :n], in_=ps1[:, :n], func=AF.Copy, scale=1.0 / D, reads=[ps1], writes=[lb.m])
    p.dve("tensor_tensor", out=lb.msq[:, :n], in0=lb.m[:, :n], in1=lb.m[:, :n], op=ALU.mult, reads=[lb.m], writes=[lb.msq])
    p.dve("scalar_tensor_tensor", out=lb.r[:, :n], in0=ps2[:, :n], scalar=1.0 / D, in1=lb.msq[:, :n],
          op0=ALU.mult, op1=ALU.subtract, reads=[ps2, lb.msq], writes=[lb.r])
    p.dve("tensor_scalar", out=lb.r[:, :n], in0=lb.r[:, :n], scalar1=LN_EPS, scalar2=None, op0=ALU.add,
          reads=[lb.r], writes=[lb.r])
    p.act("activation", out=lb.r[:, :n], in_=lb.r[:, :n], func=AF.Ln, reads=[lb.r], writes=[lb.r])
    p.act("activation", out=lb.r[:, :n], in_=lb.r[:, :n], func=AF.Exp, scale=-0.5, reads=[lb.r], writes=[lb.r])
    p.dve("tensor_tensor", out=lb.mr[:, :n], in0=lb.m[:, :n], in1=lb.r[:, :n], op=ALU.mult, reads=[lb.m, lb.r], writes=[lb.mr])
    for kc in range(KC):
        t = lb.t[kc % 2]
        t2 = lb.t2[kc % 2]
        p.dve("tensor_tensor", out=t[:, :n], in0=z[:, kc, :n], in1=lb.r[:, :n], op=ALU.mult, reads=[z, lb.r], writes=[t])
        p.pool("tensor_tensor", out=t2[:, :n], in0=t[:, :n], in1=lb.mr[:, :n], op=ALU.subtract, reads=[t, lb.mr], writes=[t2])
        if outf is not None:
            p.act("activation", out=outf[:, kc, :n], in_=t2[:, :n], func=AF.Identity,
                  scale=vecs[:, gcol + kc:gcol + kc + 1], bias=vecs[:, bcol + kc:bcol + kc + 1],
                  reads=[t2, vecs], writes=[(outf, kc)])
            if outb is not None:
                p.pool("tensor_copy", out=outb[:, kc, :n], in_=outf[:, kc, :n], reads=[(outf, kc)], writes=[(outb, kc)])
        else:
            p.act("activation", out=outb[:, kc, :n], in_=t2[:, :n], func=AF.Identity,
                  scale=vecs[:, gcol + kc:gcol + kc + 1], bias=vecs[:, bcol + kc:bcol + kc + 1],
                  reads=[t2, vecs], writes=[(outb, kc)])


def build_p0(NOUT, TW=512):
    p = Prog()
    x = p.dram_in("x", [128, KC, NOUT], F32)
    vecs_d = p.dram_in("vecs", [128, 16], F32)
    of = p.dram_out("of", [128, KC, NOUT], F32)
    ob = p.dram_out("ob", [128, KC, NOUT], BF16)
    vecs = p.sbuf("vecs", [128, 16], F32)
    ones_f = p.sbuf("ones", [128, 128], F32)
    p.dma("sync", vecs[:], vecs_d[:], reads=[vecs_d], writes=[vecs])
    p.pool("memset", ones_f[:], 1.0, writes=[ones_f])
    lb = LNBufs(p, TW)
    ps1 = p.psum("ps1", [128, 512])
    ps2 = p.psum("ps2", [128, 512])
    zs = [p.sbuf("z%d" % i, [128, KC, TW], F32) for i in range(2)]
    ofs = [p.sbuf("of%d" % i, [128, KC, TW], F32) for i in range(2)]
    obs = [p.sbuf("ob%d" % i, [128, KC, TW], BF16) for i in range(2)]
    for i, (c0, n) in enumerate(tiles_of(NOUT, TW)):
        z, o_f, o_b = zs[i % 2], ofs[i % 2], obs[i % 2]
        p.dma("sync", z[:, :, :n], x[:, :, c0:c0 + n], reads=[x], writes=[z])
        ln_tile(p, z, n, vecs, 0, 8, o_f, o_b, lb, ones_f, ps1, ps2)
        p.dma("sync", of[:, :, c0:c0 + n], o_f[:, :, :n], reads=[o_f], writes=[(of, i)])
        p.dma("sync", ob[:, :, c0:c0 + n], o_b[:, :, :n], reads=[o_b], writes=[(ob, i)])
    p.finish()
    return p


def build_p2(NOUT, TWA=171, TWB=342):
    W = NOUT + 2
    p = Prog()
    hf_d = p.dram_in("hf", [128, KC, W], F32)
    hb_d = p.dram_in("hb", [128, KC, W], BF16)
    ob_d = p.dram_in("ob", [128, KC, W], BF16)
    wg_d = p.dram_in("wg", [128, KC, 4096], F32)
    wb_d = p.dram_in("wb", [128, KC, 1024], F32)
    wo_d = p.dram_in("wo", [128, KC, 1024], F32)
    vecs_d = p.dram_in("vecs", [128, NV], F32)
    wup_d = p.dram_in("wup", [NFC, 128, KC * 256], F32)
    wdn_d = p.dram_in("wdn", [NFC, 128, 1024], F32)
    of_d = p.dram_out("of", [128, KC, NOUT], F32)
    obf_d = p.dram_out("obf", [128, KC, NOUT], BF16)
    hmid_d = p.dram_tmp("hmid_scratch", [128, KC, W], F32)

    vecs = p.sbuf("vecs", [128, NV], F32)
    ones_f = p.sbuf("ones", [128, 128], F32)
    hmid_b = p.sbuf("hmid_b", [128, KC, W], BF16)
    p.dma("sync", vecs[:], vecs_d[:], reads=[vecs_d], writes=[vecs])
    p.pool("memset", ones_f[:], 1.0, writes=[ones_f])
    PS = [p.psum("ps%d" % i, [128, 512]) for i in range(8)]
    NMAX = max(TWA + 2, TWB + 2)
    NA = TWA + 2
    lb = LNBufs(p, NMAX)
    z = p.sbuf("z", [128, KC, NMAX], F32)

    with p.phase():
        wg = p.sbuf("wg", [128, KC, 4096], BF16)
        wb = p.sbuf("wb", [128, KC, 1024], BF16)
        wo = p.sbuf("wo", [128, KC, 1024], BF16)
        for kc in range(KC):
            p.dma("gpsimd", wg[:, kc, :], wg_d[:, kc, :], reads=[wg_d], writes=[(wg, kc)])
            p.dma("gpsimd", wb[:, kc, :], wb_d[:, kc, :], reads=[wb_d], writes=[(wb, kc)])
            p.dma("gpsimd", wo[:, kc, :], wo_d[:, kc, :], reads=[wo_d], writes=[(wo, kc)])
        hbs = [p.sbuf("hb%d" % i, [128, KC, NA], BF16) for i in range(2)]
        obs = [p.sbuf("ob%d" % i, [128, KC, NA], BF16) for i in range(2)]
        hfs = [p.sbuf("hf%d" % i, [128, KC, NA], F32) for i in range(2)]
        acc = p.sbuf("acc", [128, KC, NA], F32)
        mixb = p.sbuf("mixb", [128, KC, NA], BF16)
        gsb = [p.sbuf("gsb%d" % i, [128, NA], F32) for i in range(2)]
        tmp = [p.sbuf("tmpa%d" % i, [128, NA], F32) for i in range(2)]
        hmf = [p.sbuf("hmf%d" % i, [128, KC, NA], F32) for i in range(2)]
        for ti, (c0, n) in enumerate(tiles_of(W, TWA, first=TWA + 2)):
            hb, ob, hf = hbs[ti % 2], obs[ti % 2], hfs[ti % 2]
            p.dma("sync", hb[:, :, :n], hb_d[:, :, c0:c0 + n], reads=[hb_d], writes=[hb])
            p.dma("sync", ob[:, :, :n], ob_d[:, :, c0:c0 + n], reads=[ob_d], writes=[ob])
            p.dma("sync", hf[:, :, :n], hf_d[:, :, c0:c0 + n], reads=[hf_d], writes=[hf])
            it = 0
            for nb in range(4):
                for dc in range(KC):
                    psg = PS[it % 2]
                    psp = PS[2 + it % 2]
                    g = gsb[it % 2]
                    t = tmp[it % 2]
                    it += 1
                    col = nb * 1024 + dc * 128
                    for kc in range(KC):
                        p.pe("matmul", psg[:, :n], wg[:, kc, col:col + 128], hb[:, kc, :n], start=(kc == 0), stop=(kc == KC - 1),
                             reads=[(wg, kc), hb], writes=[psg])
                    for k2 in range(2):
                        p.pe("matmul", psp[:, :n], wb[:, nb * 2 + k2, dc * 128:dc * 128 + 128], ob[:, nb * 2 + k2, :n],
                             start=(k2 == 0), stop=(k2 == 1), reads=[(wb, nb * 2 + k2), ob], writes=[psp])
                    p.act("activation", out=g[:, :n], in_=psg[:, :n], func=AF.Sigmoid, reads=[psg], writes=[g])
                    if nb == 0:
                        p.dve("tensor_tensor", out=acc[:, dc, :n], in0=g[:, :n], in1=psp[:, :n], op=ALU.mult,
                              reads=[g, psp], writes=[(acc, dc)])
                    else:
                        p.dve("tensor_tensor", out=t[:, :n], in0=g[:, :n], in1=psp[:, :n], op=ALU.mult,
                              reads=[g, psp], writes=[t])
                        if nb < 3:
                            p.pool("tensor_tensor", out=acc[:, dc, :n], in0=acc[:, dc, :n], in1=t[:, :n], op=ALU.add,
                                   reads=[(acc, dc), t], writes=[(acc, dc)])
                        else:
                            p.pool("tensor_tensor", out=mixb[:, dc, :n], in0=acc[:, dc, :n], in1=t[:, :n], op=ALU.add,
                                   reads=[(acc, dc), t], writes=[(mixb, dc)])
            for dc in range(KC):
                pso = PS[4 + dc % 2]
                for kc in range(KC):
                    p.pe("matmul", pso[:, :n], wo[:, kc, dc * 128:dc * 128 + 128], mixb[:, kc, :n], start=(kc == 0), stop=(kc == KC - 1),
                         reads=[(wo, kc), mixb], writes=[pso])
                p.dve("scalar_tensor_tensor", out=z[:, dc, :n], in0=hf[:, dc, :n], scalar=ALPHA, in1=pso[:, :n],
                      op0=ALU.mult, op1=ALU.add, reads=[hf, pso], writes=[(z, dc)])
            hm = hmf[ti % 2]
            ln_tile(p, z, n, vecs, V_LNMIX_G, V_LNMIX_B, hm, None, lb, ones_f, PS[6], PS[7])
            for kc in range(KC):
                p.pool("tensor_copy", out=hmid_b[:, kc, c0:c0 + n], in_=hm[:, kc, :n], reads=[(hm, kc)], writes=[(hmid_b, ("a", ti))])
            p.dma("sync", hmid_d[:, :, c0:c0 + n], hm[:, :, :n], reads=[hm], writes=[(hmid_d, ti)])
        for kc in range(KC):
            p.dve("tensor_scalar", out=hmid_b[:, kc, 0:2], in0=hmid_b[:, kc, 0:2], scalar1=vecs[:, V_FLAG:V_FLAG + 1], scalar2=None,
                  op0=ALU.mult, reads=[hmid_b, vecs], writes=[hmid_b])

    with p.phase():
        wd = p.sbuf("wd", [128, NFC, 1024], BF16)
        for fc in range(NFC):
            p.dma("gpsimd", wd[:, fc, :], wdn_d[fc], reads=[wdn_d], writes=[(wd, fc)])
        wus = [p.sbuf("wus%d" % i, [128, KC * 256], F32) for i in range(2)]
        wub = [p.sbuf("wub%d" % i, [128, KC, 256], BF16) for i in range(2)]
        actb = p.sbuf("actb", [128, NFC, TWB], BF16)
        t1 = [p.sbuf("t1_%d" % i, [128, TWB], F32) for i in range(2)]
        t2 = [p.sbuf("t2_%d" % i, [128, TWB], F32) for i in range(2)]
        sg = [p.sbuf("sg_%d" % i, [128, TWB], F32) for i in range(2)]
        hmf = p.sbuf("hmfb", [128, KC, TWB], F32)
        outf = p.sbuf("outf", [128, KC, TWB], F32)
        outb = p.sbuf("outb", [128, KC, TWB], BF16)
        it = 0
        for ti, (o0, no) in enumerate(tiles_of(NOUT, TWB)):
            nin = no + 2
            p.dma("sync", hmf[:, :, :no], hmid_d[:, :, o0 + 2:o0 + 2 + no], reads=[hmid_d], writes=[hmf])
            for fc in range(NFC):
                ws, wbb = wus[it % 2], wub[it % 2]
                p.dma("sync", ws[:], wup_d[fc], reads=[wup_d], writes=[ws])
                p.pool("tensor_copy", out=wbb[:].rearrange("p k c -> p (k c)"), in_=ws[:], reads=[ws], writes=[wbb])
                for half in range(2):
                    psu = PS[(2 * it + half) % 4]
                    for kc in range(KC):
                        p.pe("matmul", psu[:, :nin], wbb[:, kc, half * 128:half * 128 + 128], hmid_b[:, kc, o0:o0 + nin],
                             start=(kc == 0), stop=(kc == KC - 1), reads=[wbb, hmid_b], writes=[psu])
                    ch = fc + half * NFC
                    tt = t1[it % 2] if half == 0 else t2[it % 2]
                    cw = lambda k: vecs[:, V_CW + k * 44 + ch:V_CW + k * 44 + ch + 1]
                    p.act("activation", out=tt[:, :no], in_=psu[:, 2:2 + no], func=AF.Identity, scale=cw(2),
                          bias=vecs[:, V_CB + ch:V_CB + ch + 1], reads=[psu, vecs], writes=[tt])
                    p.dve("scalar_tensor_tensor", out=tt[:, :no], in0=psu[:, 1:1 + no], scalar=cw(1), in1=tt[:, :no],
                          op0=ALU.mult, op1=ALU.add, reads=[psu, tt, vecs], writes=[tt])
                    p.dve("scalar_tensor_tensor", out=tt[:, :no], in0=psu[:, 0:no], scalar=cw(0), in1=tt[:, :no],
                          op0=ALU.mult, op1=ALU.add, reads=[psu, tt, vecs], writes=[tt])
                s_ = sg[it % 2]
                p.act("activation", out=s_[:, :no], in_=t1[it % 2][:, :no], func=AF.Silu, reads=[t1[it % 2]], writes=[s_])
                p.pool("tensor_tensor", out=actb[:, fc, :no], in0=s_[:, :no], in1=t2[it % 2][:, :no], op=ALU.mult,
                       reads=[s_, t2[it % 2]], writes=[(actb, fc)])
                it += 1
            for dc in range(KC):
                psd = PS[4 + dc % 2]
                for fc in range(NFC):
                    p.pe("matmul", psd[:, :no], wd[:, fc, dc * 128:dc * 128 + 128], actb[:, fc, :no], start=(fc == 0), stop=(fc == NFC - 1),
                         reads=[(wd, fc), (actb, fc)], writes=[psd])
                p.dve("scalar_tensor_tensor", out=z[:, dc, :no], in0=hmf[:, dc, :no], scalar=ALPHA, in1=psd[:, :no],
                      op0=ALU.mult, op1=ALU.add, reads=[hmf, psd], writes=[(z, dc)])
            ln_tile(p, z, no, vecs, V_LNFFN_G, V_LNFFN_B, outf, outb, lb, ones_f, PS[6], PS[7])
            p.dma("sync", of_d[:, :, o0:o0 + no], outf[:, :, :no], reads=[outf], writes=[(of_d, ti)])
            p.dma("sync", obf_d[:, :, o0:o0 + no], outb[:, :, :no], reads=[outb], writes=[(obf_d, ti)])
    p.finish()
    return p


NEG = -30000.0
QW = 512


def make_consts(p):
    c = {}
    ident = p.sbuf("ident", [128, 128], F32)
    p.pool("memset", ident[:], 0.0, writes=[ident])
    p.pool("affine_select", out=ident[:], in_=ident[:], pattern=[[-1, 128]], compare_op=ALU.not_equal, fill=1.0,
           base=0, channel_multiplier=1, reads=[ident], writes=[ident])
    identb = p.sbuf("identb", [128, 128], BF16)
    p.dve("tensor_copy", out=identb[:], in_=ident[:], reads=[ident], writes=[identb])
    ones = p.sbuf("ones", [128, 128], F32)
    p.pool("memset", ones[:], 1.0, writes=[ones])
    maskf = p.sbuf("maskf", [128, 4, QW], F32)
    maskb = p.sbuf("maskb", [128, 4, QW], BF16)
    p.pool("memset", maskf[:], 0.0, writes=[maskf])
    for j in range(4):
        p.pool("affine_select", out=maskf[:, j, :], in_=maskf[:, j, :], pattern=[[1, QW]], compare_op=ALU.is_ge, fill=NEG,
               base=-128 * j, channel_multiplier=-1, reads=[maskf], writes=[maskf])
    p.dve("tensor_copy", out=maskb[:], in_=maskf[:], reads=[maskf], writes=[maskb])
    c.update(ident=ident, identb=identb, ones=ones, maskb=maskb)
    return c


def attn_core(p, c, PS, qT, kT, Vaug, kbias, T, out_d, KA):
    NKT = (T + 127) // 128
    pT = [p.sbuf("pT%d" % i, [128, QW], BF16) for i in range(3)]
    osb = p.sbuf("osb", [65, QW], F32)
    rden = p.sbuf("rden", [64, QW], F32)
    ob = [p.sbuf("ob%d" % i, [64, QW], BF16) for i in range(2)]
    sel = p.sbuf("sel", [65, 64], F32)
    p.pool("memset", sel[:], 0.0, writes=[sel])
    p.pool("memset", sel[64:65, :], 1.0, reads=[sel], writes=[sel])
    zero_b = p.sbuf("zero_b", [128, 1], F32)
    p.pool("memset", zero_b[:], 0.0, writes=[zero_b])
    it = 0
    for qi, (q0, nq) in enumerate(tiles_of(T, QW)):
        pso = PS[4 + qi % 2]
        kt_last = (q0 + nq - 1) // 128
        for kt in range(kt_last + 1):
            nk = min(128, T - kt * 128)
            pss = PS[it % 3]
            pt = pT[it % 3]
            it += 1
            j = kt - q0 // 128
            diag = j >= 0
            p.pe("matmul", pss[:nk, :nq], kT[:KA, kt * 128:kt * 128 + nk], qT[:KA, q0:q0 + nq], start=True, stop=not diag,
                 reads=[kT, qT], writes=[pss])
            if diag:
                p.pe("matmul", pss[:nk, :nq], c["identb"][:, :nk], c["maskb"][:, j, :nq], start=False, stop=True,
                     reads=[c["identb"], c["maskb"]], writes=[pss])
            bias = kbias[:nk, kt:kt + 1] if kbias is not None else zero_b[:nk, :]
            p.act("activation", out=pt[:nk, :nq], in_=pss[:nk, :nq], func=AF.Exp, bias=bias,
                  reads=[pss] + ([kbias] if kbias is not None else [zero_b]), writes=[pt])
            p.pe("matmul", pso[:65, :nq], Vaug[:nk, kt, :], pt[:nk, :nq], start=(kt == 0), stop=(kt == kt_last),
                 reads=[Vaug, pt], writes=[pso])
        p.act("activation", out=osb[:, :nq], in_=pso[:65, :nq], func=AF.Copy, reads=[pso], writes=[osb])
        psd = PS[6]
        p.pe("matmul", psd[:64, :nq], sel[:, :], osb[:, :nq], start=True, stop=True, reads=[sel, osb], writes=[psd])
        p.dve("reciprocal", out=rden[:, :nq], in_=psd[:64, :nq], reads=[psd], writes=[rden])
        o = ob[qi % 2]
        p.dve("tensor_tensor", out=o[:, :nq], in0=osb[0:64, :nq], in1=rden[:, :nq], op=ALU.mult, reads=[osb, rden], writes=[o])
        p.dma("sync", out_d[:, q0:q0 + nq], o[:, :nq], reads=[o], writes=[(out_d, qi)])


def emit_fox(p, c, PS, hb_d, T, W, out_d, TW=512, stop_after=99):
    NKT = (T + 127) // 128
    FR = slice(64, 128)
    r = slice(0, 1)
    with p.phase():
        wq = p.sbuf("wq", [128, KC, 128], BF16)
        wk = p.sbuf("wk", [128, KC, 128], BF16)
        wv = p.sbuf("wv", [128, KC, 64], BF16)
        p.dma("gpsimd", wq[:], W["wq"][:], reads=[W["wq"]], writes=[wq])
        p.dma("gpsimd", wk[:], W["wk"][:], reads=[W["wk"]], writes=[wk])
        p.dma("gpsimd", wv[:], W["wv"][:], reads=[W["wv"]], writes=[wv])
        fv = p.sbuf("fv", [1, 2], F32)
        p.dma("sync", fv[:], W["fv"][:], reads=[W["fv"]], writes=[fv])
        nbf = p.sbuf("nbf", [1, 1], F32)
        p.dve("tensor_scalar", out=nbf[:, :], in0=fv[:, 0:1], scalar1=-1.0, scalar2=None, op0=ALU.mult, reads=[fv], writes=[nbf])
        qT = p.sbuf("qT", [128, T], BF16)
        kT = p.sbuf("kT", [128, T], BF16)
        Vaug = p.sbuf("Vaug", [128, NKT, 65], BF16)
        A = p.sbuf("rowA", [1, T], F32)
        Bq = p.sbuf("rowB", [1, T], F32)
        ntile = len(tiles_of(T, TW))
        kmx = p.sbuf("kmx", [1, ntile + 1], F32)
        e1 = p.sbuf("e1", [128, 1], F32)
        p.pool("memset", e1[:], 1.0, writes=[e1])
        p.pool("memset", e1[0:64, :], 0.0, reads=[e1], writes=[e1])
        p.pool("memset", kT[0:64, :], 0.0, writes=[kT])
        p.pool("memset", kT[0:1, :], 1.0, reads=[kT], writes=[kT])
        p.pool("memset", Vaug[:], 1.0, writes=[Vaug])
        hbs = [p.sbuf("hb%d" % i, [128, KC, TW], BF16) for i in range(2)]
        qf = p.sbuf("qf", [128, TW], F32)
        sq = p.sbuf("sq", [128, TW], F32)
        p.pool("memset", sq[:], 0.0, writes=[sq])
        scale = 1.0 / 8.0
        for ti, (c0, n) in enumerate(tiles_of(T, TW)):
            hb = hbs[ti % 2]
            p.dma("sync", hb[:, :, :n], hb_d[:, :, c0:c0 + n], reads=[hb_d], writes=[hb])
            psq, psk, psn = PS[0], PS[1], PS[2]
            for kc in range(KC):
                p.pe("matmul", psq[:, :n], wq[:, kc, :], hb[:, kc, :n], start=(kc == 0), stop=(kc == KC - 1), reads=[wq, hb], writes=[psq])
            for kc in range(KC):
                p.pe("matmul", psk[:, :n], wk[:, kc, :], hb[:, kc, :n], start=(kc == 0), stop=(kc == KC - 1), reads=[wk, hb], writes=[psk])
            p.act("activation", out=qf[:, :n], in_=psq[:, :n], func=AF.Copy, scale=scale, reads=[psq], writes=[qf])
            p.act("activation", out=A[:, c0:c0 + n], in_=psq[0:1, :n], func=AF.Copy, reads=[psq], writes=[(A, ti)])
            p.pool("tensor_copy", out=qT[:, c0:c0 + n], in_=qf[:, :n], reads=[qf], writes=[(qT, ti)])
            p.dve("tensor_tensor", out=sq[FR, :n], in0=qf[FR, :n], in1=qf[FR, :n], op=ALU.mult, reads=[qf], writes=[sq])
            p.pe("matmul", psn[:1, :n], e1[:, :], sq[:, :n], start=True, stop=True, reads=[e1, sq], writes=[psn])
            p.dve("tensor_copy", out=Bq[:, c0:c0 + n], in_=psn[:1, :n], reads=[psn], writes=[(Bq, ti)])
            p.act("activation", out=kT[FR, c0:c0 + n], in_=psk[FR, :n], func=AF.Copy, reads=[psk], writes=[(kT, ti)])
            p.act("activation", out=sq[FR, :n], in_=psk[FR, :n], func=AF.Square, reads=[psk], writes=[sq])
            p.pe("matmul", psn[:1, :n], e1[:, :], sq[:, :n], start=True, stop=True, reads=[e1, sq], writes=[psn])
            p.dve("tensor_reduce", out=kmx[:, ti:ti + 1], in_=psn[:1, :n], axis=AX.X, op=ALU.max, reads=[psn], writes=[(kmx, ti)])
            for s0 in range(0, n, 128):
                ns = min(128, n - s0)
                kt = (c0 + s0) // 128
                psv = PS[3 + (s0 // 128) % 2]
                for kc in range(KC):
                    p.pe("matmul", psv[:ns, :64], hb[:, kc, s0:s0 + ns], wv[:, kc, :], start=(kc == 0), stop=(kc == KC - 1),
                         reads=[hb, wv], writes=[psv])
                p.dve("tensor_copy", out=Vaug[:ns, kt, 0:64], in_=psv[:ns, :64], reads=[psv], writes=[(Vaug, kt)])
        if stop_after < 1:
            return
        p.dve("tensor_reduce", out=kmx[:, ntile:ntile + 1], in_=kmx[:, 0:ntile], axis=AX.X, op=ALU.max, reads=[kmx], writes=[kmx])
        p.act("activation", out=Bq[:, :], in_=Bq[:, :], func=AF.Sqrt, scale=kmx[:, ntile:ntile + 1], reads=[Bq, kmx], writes=[Bq])
        p.act("activation", out=A[:, :], in_=A[:, :], func=AF.Exp, scale=-1.0, bias=nbf[:, :], reads=[A, nbf], writes=[A])
        p.dve("tensor_scalar", out=A[:, :], in0=A[:, :], scalar1=1.0, scalar2=None, op0=ALU.add, reads=[A], writes=[A])
        p.act("activation", out=A[:, :], in_=A[:, :], func=AF.Ln, reads=[A], writes=[A])
        p.dve("tensor_scalar", out=A[:, :], in0=A[:, :], scalar1=-1.0, scalar2=None, op0=ALU.mult, reads=[A], writes=[A])
        if stop_after < 2:
            return
        F_ = p.sbuf("rowF", [1, T], F32)
        p.dve("tensor_tensor_scan", out=F_[:, :], data0=A[:, :], data1=A[:, :], initial=0.0, op0=ALU.add, op1=ALU.min, reads=[A], writes=[F_])
        p.dve("tensor_tensor", out=qT[0:1, :], in0=F_[:, :], in1=Bq[:, :], op=ALU.subtract, reads=[F_, Bq], writes=[qT])
        if stop_after < 3:
            return
        kbias = p.sbuf("kbias", [128, NKT], F32)
        pst = PS[0]
        for kt in range(NKT):
            nk = min(128, T - kt * 128)
            p.pe("transpose", pst[:nk, kt:kt + 1], F_[:, kt * 128:kt * 128 + nk], c["ident"][0:1, 0:1], reads=[F_, c["ident"]], writes=[pst])
        p.dve("memset", kbias[:], 0.0, writes=[kbias])
        full = T // 128
        p.dve("tensor_scalar", out=kbias[:, 0:full], in0=pst[:, 0:full], scalar1=-1.0, scalar2=None, op0=ALU.mult, reads=[pst, kbias], writes=[kbias])
        if T % 128:
            nk = T % 128
            p.dve("tensor_scalar", out=kbias[:nk, full:full + 1], in0=pst[:nk, full:full + 1], scalar1=-1.0, scalar2=None, op0=ALU.mult,
                  reads=[pst, kbias], writes=[kbias])
        if stop_after < 4:
            return
        attn_core(p, c, PS, qT, kT, Vaug, kbias, T, out_d, 128)


RMS_EPS = 1e-6


def emit_mla(p, c, PS, hb_d, T, W, out_d, TW=512, stop_after=99):
    NKT = (T + 127) // 128
    scale = 1.0 / math.sqrt(96.0)
    with p.phase():
        wcq = p.sbuf("wcq", [128, KC, 256], BF16)
        wckv = p.sbuf("wckv", [128, KC, 128], BF16)
        wkr = p.sbuf("wkr", [128, KC, 64], BF16)
        wkrs = p.sbuf("wkrs", [128, KC, 64], BF16)
        for nm, t in (("wcq", wcq), ("wckv", wckv), ("wkr", wkr), ("wkrs", wkrs)):
            p.dma("gpsimd", t[:], W[nm][:], reads=[W[nm]], writes=[t])
        gv = p.sbuf("gv", [128, 3], F32)
        p.dma("sync", gv[:], W["gv"][:], reads=[W["gv"]], writes=[gv])
        stg = p.sbuf("stg", [128, 2, 128], F32)
        wuq = p.sbuf("wuq", [128, 2, 128], BF16)
        wuqs = p.sbuf("wuqs", [128, 2, 128], BF16)
        for nm, t in (("wuq", wuq), ("wuqs", wuqs)):
            p.dma("sync", stg[:], W[nm][:], reads=[W[nm]], writes=[stg])
            for c2 in range(2):
                p.dve("tensor_scalar", out=t[:, c2, :], in0=stg[:, c2, :], scalar1=gv[:, c2:c2 + 1], scalar2=None, op0=ALU.mult,
                      reads=[stg, gv], writes=[t])
        stg2 = p.sbuf("stg2", [128, 192], F32)
        wukv = p.sbuf("wukv", [128, 192], BF16)
        p.dma("sync", stg2[:], W["wukv"][:], reads=[W["wukv"]], writes=[stg2])
        p.dve("tensor_scalar", out=wukv[:], in0=stg2[:], scalar1=gv[:, 2:3], scalar2=None, op0=ALU.mult, reads=[stg2, gv], writes=[wukv])
        qT = p.sbuf("qT", [128, T], BF16)
        kT = p.sbuf("kT", [128, T], BF16)
        Vaug = p.sbuf("Vaug", [128, NKT, 65], BF16)
        Bq = p.sbuf("rowB", [1, T], F32)
        ntile = len(tiles_of(T, TW))
        kmx = p.sbuf("kmx", [1, ntile + 1], F32)
        e96 = p.sbuf("e96", [128, 1], F32)
        p.pool("memset", e96[:], 1.0, writes=[e96])
        p.pool("memset", e96[0:32, :], 0.0, reads=[e96], writes=[e96])
        p.pool("memset", kT[0:32, :], 0.0, writes=[kT])
        p.pool("memset", kT[0:1, :], 1.0, reads=[kT], writes=[kT])
        p.pool("memset", Vaug[:], 1.0, writes=[Vaug])
        hbs = [p.sbuf("hb%d" % i, [128, KC, TW], BF16) for i in range(2)]
        cqb = p.sbuf("cqb", [128, 2, TW], BF16)
        cqsq = p.sbuf("cqsq", [128, 2, TW], F32)
        ckvb = p.sbuf("ckvb", [128, TW], BF16)
        ckvsq = p.sbuf("ckvsq", [128, TW], F32)
        rq = p.sbuf("rq", [128, TW], F32)
        rkv = p.sbuf("rkv", [128, TW], F32)
        qh = p.sbuf("qh", [128, TW], F32)
        qs = p.sbuf("qs", [128, TW], F32)
        kf = p.sbuf("kf", [128, TW], F32)
        ks = p.sbuf("ks", [128, TW], F32)
        sq = p.sbuf("sq", [128, TW], F32)
        cs2 = p.sbuf("cs2", [64, TW], F32)
        sn2 = p.sbuf("sn2", [64, TW], F32)
        rtok = p.sbuf("rtok", [128, 4], F32)
        R = slice(32, 64)
        NP = slice(64, 128)
        p.pool("memset", kf[:], 0.0, writes=[kf])
        for ti, (c0, n) in enumerate(tiles_of(T, TW)):
            hb = hbs[ti % 2]
            p.dma("sync", hb[:, :, :n], hb_d[:, :, c0:c0 + n], reads=[hb_d], writes=[hb])
            if stop_after < -1:
                continue
            p.dma("sync", cs2[R, :n], W["cos2"][:, c0:c0 + n], reads=[W["cos2"]], writes=[cs2])
            p.dma("sync", sn2[R, :n], W["sin2"][:, c0:c0 + n], reads=[W["sin2"]], writes=[sn2])
            if stop_after < 0:
                continue
            for c2 in range(2):
                for kc in range(KC):
                    p.pe("matmul", PS[c2][:, :n], wcq[:, kc, c2 * 128:c2 * 128 + 128], hb[:, kc, :n], start=(kc == 0), stop=(kc == KC - 1),
                         reads=[wcq, hb], writes=[PS[c2]])
            for kc in range(KC):
                p.pe("matmul", PS[2][:, :n], wckv[:, kc, :], hb[:, kc, :n], start=(kc == 0), stop=(kc == KC - 1), reads=[wckv, hb], writes=[PS[2]])
            for kc in range(KC):
                p.pe("matmul", PS[3][:64, :n], wkr[:, kc, :], hb[:, kc, :n], start=(kc == 0), stop=(kc == KC - 1), reads=[wkr, hb], writes=[PS[3]])
            for kc in range(KC):
                p.pe("matmul", PS[4][:64, :n], wkrs[:, kc, :], hb[:, kc, :n], start=(kc == 0), stop=(kc == KC - 1), reads=[wkrs, hb], writes=[PS[4]])
            if stop_after < 0.2:
                continue
            for c2 in range(2):
                p.dve("tensor_copy", out=cqb[:, c2, :n], in_=PS[c2][:, :n], reads=[PS[c2]], writes=[(cqb, c2)])
                p.act("activation", out=cqsq[:, c2, :n], in_=PS[c2][:, :n], func=AF.Square, reads=[PS[c2]], writes=[(cqsq, c2)])
            p.dve("tensor_copy", out=ckvb[:, :n], in_=PS[2][:, :n], reads=[PS[2]], writes=[ckvb])
            p.act("activation", out=ckvsq[:, :n], in_=PS[2][:, :n], func=AF.Square, reads=[PS[2]], writes=[ckvsq])
            if stop_after < 0.4:
                continue
            for c2 in range(2):
                p.pe("matmul", PS[5][:, :n], c["ones"][:], cqsq[:, c2, :n], start=(c2 == 0), stop=(c2 == 1), reads=[c["ones"], (cqsq, c2)], writes=[PS[5]])
            p.pe("matmul", PS[6][:, :n], c["ones"][:], ckvsq[:, :n], start=True, stop=True, reads=[c["ones"], ckvsq], writes=[PS[6]])
            if stop_after < 0.6:
                continue
            for ps_, rr, dim in ((PS[5], rq, 256.0), (PS[6], rkv, 128.0)):
                p.dve("tensor_scalar", out=rr[:, :n], in0=ps_[:, :n], scalar1=1.0 / dim, scalar2=RMS_EPS, op0=ALU.mult, op1=ALU.add,
                      reads=[ps_], writes=[rr])
                p.act("activation", out=rr[:, :n], in_=rr[:, :n], func=AF.Ln, reads=[rr], writes=[rr])
                p.act("activation", out=rr[:, :n], in_=rr[:, :n], func=AF.Exp, scale=-0.5, reads=[rr], writes=[rr])
            if stop_after < 1:
                continue
            p.dve("tensor_tensor", out=kf[R, :n], in0=PS[3][R, :n], in1=cs2[R, :n], op=ALU.mult, reads=[PS[3], cs2], writes=[kf])
            p.dve("tensor_tensor", out=ks[R, :n], in0=PS[4][R, :n], in1=sn2[R, :n], op=ALU.mult, reads=[PS[4], sn2], writes=[ks])
            p.pool("tensor_tensor", out=kf[R, :n], in0=kf[R, :n], in1=ks[R, :n], op=ALU.add, reads=[kf, ks], writes=[kf])
            for c2 in range(2):
                p.pe("matmul", PS[0][:, :n], wuq[:, c2, :], cqb[:, c2, :n], start=(c2 == 0), stop=(c2 == 1), reads=[wuq, (cqb, c2)], writes=[PS[0]])
            for c2 in range(2):
                p.pe("matmul", PS[1][:, :n], wuqs[:, c2, :], cqb[:, c2, :n], start=(c2 == 0), stop=(c2 == 1), reads=[wuqs, (cqb, c2)], writes=[PS[1]])
            p.dve("scalar_tensor_tensor", out=qh[:, :n], in0=PS[0][:, :n], scalar=scale, in1=rq[:, :n], op0=ALU.mult, op1=ALU.mult,
                  reads=[PS[0], rq], writes=[qh])
            p.dve("scalar_tensor_tensor", out=qs[:, :n], in0=PS[1][:, :n], scalar=scale, in1=rq[:, :n], op0=ALU.mult, op1=ALU.mult,
                  reads=[PS[1], rq], writes=[qs])
            p.pool("tensor_tensor", out=qh[R, :n], in0=qh[R, :n], in1=cs2[R, :n], op=ALU.mult, reads=[qh, cs2], writes=[qh])
            p.pool("tensor_tensor", out=qs[R, :n], in0=qs[R, :n], in1=sn2[R, :n], op=ALU.mult, reads=[qs, sn2], writes=[qs])
            p.pool("tensor_tensor", out=qh[R, :n], in0=qh[R, :n], in1=qs[R, :n], op=ALU.add, reads=[qh, qs], writes=[qh])
            p.act("activation", out=qT[:, c0:c0 + n], in_=qh[:, :n], func=AF.Copy, reads=[qh], writes=[(qT, ti)])
            p.dve("tensor_tensor", out=sq[:, :n], in0=qh[:, :n], in1=qh[:, :n], op=ALU.mult, reads=[qh], writes=[sq])
            p.pe("matmul", PS[7][:1, :n], e96[:, :], sq[:, :n], start=True, stop=True, reads=[e96, sq], writes=[PS[7]])
            p.dve("tensor_copy", out=Bq[:, c0:c0 + n], in_=PS[7][:1, :n], reads=[PS[7]], writes=[(Bq, ti)])
            p.pe("matmul", PS[0][:, :n], wukv[:, 0:128], ckvb[:, :n], start=True, stop=True, reads=[wukv, ckvb], writes=[PS[0]])
            p.dve("tensor_tensor", out=kf[NP, :n], in0=PS[0][NP, :n], in1=rkv[NP, :n], op=ALU.mult, reads=[PS[0], rkv], writes=[kf])
            p.act("activation", out=kT[R, c0:c0 + n], in_=kf[R, :n], func=AF.Copy, reads=[kf], writes=[(kT, ti)])
            p.act("activation", out=kT[NP, c0:c0 + n], in_=kf[NP, :n], func=AF.Copy, reads=[kf], writes=[(kT, ti)])
            p.act("activation", out=sq[:, :n], in_=kf[:, :n], func=AF.Square, reads=[kf], writes=[sq])
            p.pe("matmul", PS[7][:1, :n], e96[:, :], sq[:, :n], start=True, stop=True, reads=[e96, sq], writes=[PS[7]])
            p.dve("tensor_reduce", out=kmx[:, ti:ti + 1], in_=PS[7][:1, :n], axis=AX.X, op=ALU.max, reads=[PS[7]], writes=[(kmx, ti)])
            if stop_after < 2:
                continue
            nsub = (n + 127) // 128
            p.dve("memset", rtok[:], 1.0, writes=[rtok])
            for si in range(nsub):
                s0 = si * 128
                ns = min(128, n - s0)
                p.pe("matmul", PS[1][:ns, si:si + 1], ckvsq[:, s0:s0 + ns], c["ones"][:, 0:1], start=True, stop=True,
                     reads=[ckvsq, c["ones"]], writes=[PS[1]])
                p.dve("tensor_scalar", out=rtok[:ns, si:si + 1], in0=PS[1][:ns, si:si + 1], scalar1=1.0 / 128.0, scalar2=RMS_EPS,
                      op0=ALU.mult, op1=ALU.add, reads=[PS[1], rtok], writes=[rtok])
            p.act("activation", out=rtok[:, :nsub], in_=rtok[:, :nsub], func=AF.Ln, reads=[rtok], writes=[rtok])
            p.act("activation", out=rtok[:, :nsub], in_=rtok[:, :nsub], func=AF.Exp, scale=-0.5, reads=[rtok], writes=[rtok])
            for si in range(nsub):
                s0 = si * 128
                ns = min(128, n - s0)
                kt = (c0 + s0) // 128
                psv = PS[2 + si % 2]
                p.pe("matmul", psv[:ns, :64], ckvb[:, s0:s0 + ns], wukv[:, 128:192], start=True, stop=True, reads=[ckvb, wukv], writes=[psv])
                p.dve("tensor_scalar", out=Vaug[:ns, kt, 0:64], in0=psv[:ns, :64], scalar1=rtok[:ns, si:si + 1], scalar2=None, op0=ALU.mult,
                      reads=[psv, rtok], writes=[(Vaug, kt)])
        if stop_after < 3:
            return
        r = slice(0, 1)
        p.dve("tensor_reduce", out=kmx[r, ntile:ntile + 1], in_=kmx[r, 0:ntile], axis=AX.X, op=ALU.max, reads=[kmx], writes=[kmx])
        p.act("activation", out=Bq[r, :], in_=Bq[r, :], func=AF.Sqrt, scale=kmx[r, ntile:ntile + 1], reads=[Bq, kmx], writes=[Bq])
        p.dve("tensor_scalar", out=qT[r, :], in0=Bq[r, :], scalar1=-1.0, scalar2=None, op0=ALU.mult, reads=[Bq], writes=[qT])
        attn_core(p, c, PS, qT, kT, Vaug, None, T, out_d, 128)


def emit_lru(p, c, PS, hb_d, T, W, out_d, TW=512, SEG=2052):
    with p.phase():
        wy = p.sbuf("wy", [128, KC, 64], BF16)
        wx = p.sbuf("wx", [128, KC, 64], BF16)
        wrg = p.sbuf("wrg", [64, 64], BF16)
        wig = p.sbuf("wig", [64, 64], BF16)
        for nm, t in (("wy", wy), ("wx", wx), ("wrg", wrg), ("wig", wig)):
            p.dma("gpsimd", t[:], W[nm][:], reads=[W[nm]], writes=[t])
        lv = p.sbuf("lv", [64, 8], F32)
        p.dma("sync", lv[:], W["lv"][:], reads=[W["lv"]], writes=[lv])
        cc = p.sbuf("cc", [64, 4], F32)
        p.act("activation", out=cc[:, 0:1], in_=lv[:, 7:8], func=AF.Exp, scale=-1.0, reads=[lv], writes=[cc])
        p.dve("tensor_scalar", out=cc[:, 0:1], in0=cc[:, 0:1], scalar1=1.0, scalar2=None, op0=ALU.add, reads=[cc], writes=[cc])
        p.act("activation", out=cc[:, 0:1], in_=cc[:, 0:1], func=AF.Ln, reads=[cc], writes=[cc])
        p.dve("tensor_scalar", out=cc[:, 1:2], in0=cc[:, 0:1], scalar1=-8.0, scalar2=None, op0=ALU.mult, reads=[cc], writes=[cc])
        p.dve("tensor_scalar", out=cc[:, 2:3], in0=cc[:, 0:1], scalar1=-16.0, scalar2=None, op0=ALU.mult, reads=[cc], writes=[cc])
        hbs = [p.sbuf("hb%d" % i, [128, KC, TW], BF16) for i in range(2)]
        xpad = p.sbuf("xpad", [64, SEG + 3], F32)
        y = p.sbuf("y", [64, SEG], F32)
        xc = p.sbuf("xc", [64, SEG], F32)
        xcb = p.sbuf("xcb", [64, SEG], BF16)
        rr = p.sbuf("rr", [64, SEG], F32)
        ii = p.sbuf("ii", [64, SEG], F32)
        aa = p.sbuf("aa", [64, SEG], F32)
        uu = p.sbuf("uu", [64, SEG], F32)
        hh = p.sbuf("hh", [64, SEG], F32)
        ob = p.sbuf("ob", [64, SEG], BF16)
        carry = p.sbuf("carry", [64, 1], F32)
        p.pool("memset", xpad[:, 0:3], 0.0, writes=[xpad])
        p.pool("memset", carry[:], 0.0, writes=[carry])
        ti = 0
        for si, (s0, L) in enumerate(tiles_of(T, SEG)):
            for (c0, n) in tiles_of(L, TW):
                hb = hbs[ti % 2]
                ti += 1
                p.dma("sync", hb[:, :, :n], hb_d[:, :, s0 + c0:s0 + c0 + n], reads=[hb_d], writes=[hb])
                for kc in range(KC):
                    p.pe("matmul", PS[0][:64, :n], wy[:, kc, :], hb[:, kc, :n], start=(kc == 0), stop=(kc == KC - 1), reads=[wy, hb], writes=[PS[0]])
                for kc in range(KC):
                    p.pe("matmul", PS[1][:64, :n], wx[:, kc, :], hb[:, kc, :n], start=(kc == 0), stop=(kc == KC - 1), reads=[wx, hb], writes=[PS[1]])
                p.act("activation", out=y[:, c0:c0 + n], in_=PS[0][:64, :n], func=AF.Copy, reads=[PS[0]], writes=[y])
                p.dve("tensor_copy", out=xpad[:, 3 + c0:3 + c0 + n], in_=PS[1][:64, :n], reads=[PS[1]], writes=[xpad])
            p.dve("tensor_scalar", out=xc[:, :L], in0=xpad[:, 3:3 + L], scalar1=lv[:, 3:4], scalar2=lv[:, 4:5], op0=ALU.mult, op1=ALU.add,
                  reads=[xpad, lv], writes=[xc])
            for k in range(3):
                p.dve("scalar_tensor_tensor", out=xc[:, :L], in0=xpad[:, k:k + L], scalar=lv[:, k:k + 1], in1=xc[:, :L], op0=ALU.mult, op1=ALU.add,
                      reads=[xpad, lv, xc], writes=[xc])
            p.pool("tensor_copy", out=xcb[:, :L], in_=xc[:, :L], reads=[xc], writes=[xcb])
            for (c0, n) in tiles_of(L, TW):
                p.pe("matmul", PS[2][:64, :n], wrg[:, :], xcb[:, c0:c0 + n], start=True, stop=True, reads=[wrg, xcb], writes=[PS[2]])
                p.pe("matmul", PS[3][:64, :n], wig[:, :], xcb[:, c0:c0 + n], start=True, stop=True, reads=[wig, xcb], writes=[PS[3]])
                p.act("activation", out=rr[:, c0:c0 + n], in_=PS[2][:64, :n], func=AF.Sigmoid, bias=lv[:, 5:6], reads=[PS[2], lv], writes=[rr])
                p.act("activation", out=ii[:, c0:c0 + n], in_=PS[3][:64, :n], func=AF.Sigmoid, bias=lv[:, 6:7], reads=[PS[3], lv], writes=[ii])
            p.act("activation", out=aa[:, :L], in_=rr[:, :L], func=AF.Exp, scale=cc[:, 1:2], reads=[rr, cc], writes=[aa])
            p.act("activation", out=uu[:, :L], in_=rr[:, :L], func=AF.Exp, scale=cc[:, 2:3], reads=[rr, cc], writes=[uu])
            p.dve("tensor_scalar", out=uu[:, :L], in0=uu[:, :L], scalar1=-1.0, scalar2=1.0, op0=ALU.mult, op1=ALU.add, reads=[uu], writes=[uu])
            p.act("activation", out=uu[:, :L], in_=uu[:, :L], func=AF.Sqrt, reads=[uu], writes=[uu])
            p.pool("tensor_tensor", out=ii[:, :L], in0=ii[:, :L], in1=xc[:, :L], op=ALU.mult, reads=[ii, xc], writes=[ii])
            p.dve("tensor_tensor", out=uu[:, :L], in0=uu[:, :L], in1=ii[:, :L], op=ALU.mult, reads=[uu, ii], writes=[uu])
            p.dve("tensor_tensor_scan", out=hh[:, :L], data0=aa[:, :L], data1=uu[:, :L], initial=carry[:, 0:1], op0=ALU.mult, op1=ALU.add,
                  reads=[aa, uu, carry], writes=[hh])
            p.dve("tensor_copy", out=carry[:, :], in_=hh[:, L - 1:L], reads=[hh], writes=[carry])
            p.dve("tensor_copy", out=xpad[:, 0:3], in_=xpad[:, L:L + 3], reads=[xpad], writes=[xpad])
            p.pool("tensor_tensor", out=rr[:, :L], in0=y[:, :L], in1=y[:, :L], op=ALU.mult, reads=[y], writes=[rr])
            p.dve("tensor_scalar", out=rr[:, :L], in0=rr[:, :L], scalar1=0.044715, scalar2=1.0, op0=ALU.mult, op1=ALU.add, reads=[rr], writes=[rr])
            p.pool("tensor_tensor", out=rr[:, :L], in0=rr[:, :L], in1=y[:, :L], op=ALU.mult, reads=[rr, y], writes=[rr])
            p.act("activation", out=rr[:, :L], in_=rr[:, :L], func=AF.Sigmoid, scale=1.5957691216057308, reads=[rr], writes=[rr])
            p.pool("tensor_tensor", out=rr[:, :L], in0=rr[:, :L], in1=y[:, :L], op=ALU.mult, reads=[rr, y], writes=[rr])
            p.dve("tensor_tensor", out=ob[:, :L], in0=rr[:, :L], in1=hh[:, :L], op=ALU.mult, reads=[rr, hh], writes=[ob])
            p.dma("sync", out_d[:, s0:s0 + L], ob[:, :L], reads=[ob], writes=[(out_d, si)])


def emit_rwkv(p, c, PS, hb_d, T, W, out_d, TW=512, SEGC=8, G=4, stop_after=99):
    C = 64
    NCH = (T + C - 1) // C
    SMAX = SEGC * C
    DEC = -math.exp(-0.5)
    ident = c["ident"]
    ones = c["ones"]
    I64 = ident[0:64, 0:64]

    def v3(buf):
        return buf[:].rearrange("p (c t) -> p c t", t=C)

    with p.phase():
        wrkv = p.sbuf("wrkv", [128, KC, 192], BF16)
        wlow = p.sbuf("wlow", [128, KC, 128], BF16)
        p.dma("gpsimd", wrkv[:], W["wrkv"][:], reads=[W["wrkv"]], writes=[wrkv])
        p.dma("gpsimd", wlow[:], W["wlow"][:], reads=[W["wlow"]], writes=[wlow])
        v64 = p.sbuf("v64", [64, 12], F32)
        v32 = p.sbuf("v32", [32, 2], F32)
        w2 = p.sbuf("w2", [32, 64], F32)
        a2 = p.sbuf("a2", [32, 64], F32)
        g2 = p.sbuf("g2", [64, 64], F32)
        gn = p.sbuf("gn", [64, 128], F32)
        for nm, t in (("v64", v64), ("v32", v32), ("w2", w2), ("a2", a2), ("g2", g2), ("gn", gn)):
            p.dma("sync", t[:], W[nm][:], reads=[W[nm]], writes=[t])
        MU_R, MU_K, MU_V, MU_G, W0, A0, KK_, KA_, RK_ = range(9)
        col = lambda t, i: t[:, i:i + 1]
        cmk = p.sbuf("cmk", [64, 320], F32)
        p.pool("memset", cmk[:], 1.0, writes=[cmk])
        for off, incl in ((0, False), (64, True), (128, False), (192, True)):
            p.pool("affine_select", out=cmk[:, off:off + 64], in_=cmk[:, off:off + 64], pattern=[[1, 64]],
                   compare_op=(ALU.is_ge if incl else ALU.is_gt), fill=0.0, base=0, channel_multiplier=-1, reads=[cmk], writes=[cmk])
        p.pool("affine_select", out=cmk[:, 256:320], in_=cmk[:, 256:320], pattern=[[-1, 64]], compare_op=ALU.is_gt, fill=0.0,
               base=0, channel_multiplier=1, reads=[cmk], writes=[cmk])
        maskS = p.sbuf("maskS", [64, SMAX], F32)
        p.pool("memset", maskS[:], 1.0, writes=[maskS])
        p.pool("memset", v3(maskS)[:, :, 0:1], 0.0, reads=[maskS], writes=[maskS])

        names6 = ("r", "k", "v", "w", "a", "g")
        parts6 = {"r": 64, "k": 64, "v": 64, "w": 32, "a": 32, "g": 64}
        pin = {n_: p.sbuf("pin_" + n_, [64, SMAX + 1], F32) for n_ in names6}
        X = {n_: p.sbuf("x_" + n_, [64, SMAX], F32) for n_ in names6}
        for n_ in names6:
            p.pool("memset", pin[n_][:, 0:1], 0.0, writes=[pin[n_]])
        mk = lambda nm: p.sbuf(nm, [64, SMAX], F32)
        sgw, aa, kk, inv, kf, LW, Wt, Wti, Wex, Bt, Kt, BWt, KWt, prod = [mk(n_) for n_ in
            ("sgw", "aa", "kk", "inv", "kf", "LW", "Wt", "Wti", "Wex", "Bt", "Kt", "BWt", "KWt", "prod")]
        AR = p.sbuf("AR", [64, SEGC, 2, 64], F32)
        GTs = p.sbuf("GTs", [64, SEGC, 64], F32)
        Hs = p.sbuf("Hs", [64, SEGC, 64], F32)
        RpTs = p.sbuf("RpTs", [64, SEGC, 64], F32)
        Yqs = p.sbuf("Yqs", [64, SEGC, 64], F32)
        Mall = p.sbuf("Mall", [64, SEGC + 1, 64], F32)
        ytok = p.sbuf("ytok", [64, SEGC, 64], F32)
        ysq = p.sbuf("ysq", [64, SEGC, 64], F32)
        otok = p.sbuf("otok", [64, SEGC, 64], F32)
        obuf = p.sbuf("obuf", [64, SMAX], BF16)
        st = p.sbuf("st", [64, 6, SEGC], F32)
        TOK = [p.sbuf("tok%d" % i, [64, G, 3, 64], F32) for i in range(2)]
        Z = [p.sbuf("Z%d" % i, [64, G, 128], F32) for i in range(2)]
        CMs = [p.sbuf("CMs%d" % i, [64, G, 320], F32) for i in range(2)]
        XN = [p.sbuf("XN%d" % i, [64, G, 128], F32) for i in range(2)]
        Ss = [p.sbuf("Ss%d" % i, [64, G, 64], F32) for i in range(2)]
        PQ = [p.sbuf("PQ%d" % i, [64, G, 128], F32) for i in range(2)]
        t1 = p.sbuf("t1", [64, 64], F32)
        t2 = p.sbuf("t2", [64, 64], F32)
        hbs = [p.sbuf("hb%d" % i, [128, KC, TW], BF16) for i in range(2)]
        p.pool("memset", Mall[:, 0, :], 0.0, writes=[Mall])
        ti = 0
        grp = 0
        for seg in range((NCH + SEGC - 1) // SEGC):
            ch0 = seg * SEGC
            nch = min(SEGC, NCH - ch0)
            S = nch * C
            tok0 = ch0 * C
            real = min(S, T - tok0)
            for (c0, n) in tiles_of(real, TW):
                hb = hbs[ti % 2]
                ti += 1
                p.dma("sync", hb[:, :, :n], hb_d[:, :, tok0 + c0:tok0 + c0 + n], reads=[hb_d], writes=[hb])
                specs = (("r", wrkv, 0, 64), ("k", wrkv, 64, 64), ("v", wrkv, 128, 64), ("w", wlow, 0, 32), ("a", wlow, 32, 32), ("g", wlow, 64, 64))
                for gi, (n_, wt, off, m) in enumerate(specs):
                    ps = PS[gi]
                    for kc in range(KC):
                        p.pe("matmul", ps[:m, :n], wt[:, kc, off:off + m], hb[:, kc, :n], start=(kc == 0), stop=(kc == KC - 1), reads=[wt, hb], writes=[ps])
                    if gi % 2 == 0:
                        p.act("activation", out=pin[n_][:m, 1 + c0:1 + c0 + n], in_=ps[:m, :n], func=AF.Copy, reads=[ps], writes=[pin[n_]])
                    else:
                        p.dve("tensor_copy", out=pin[n_][:m, 1 + c0:1 + c0 + n], in_=ps[:m, :n], reads=[ps], writes=[pin[n_]])
            if real < S:
                for n_ in names6:
                    p.pool("memset", pin[n_][:, 1 + real:1 + S], 0.0, reads=[pin[n_]], writes=[pin[n_]])
            mus = {"r": col(v64, MU_R), "k": col(v64, MU_K), "v": col(v64, MU_V), "g": col(v64, MU_G), "w": col(v32, 0), "a": col(v32, 1)}
            for n_ in names6:
                m = parts6[n_]
                p.pool("tensor_tensor", out=X[n_][:m, :S], in0=pin[n_][:m, 0:S], in1=pin[n_][:m, 1:S + 1], op=ALU.subtract, reads=[pin[n_]], writes=[X[n_]])
                p.dve("scalar_tensor_tensor", out=X[n_][:m, :S], in0=X[n_][:m, :S], scalar=mus[n_][:m, :], in1=pin[n_][:m, 1:S + 1], op0=ALU.mult, op1=ALU.add,
                      reads=[X[n_], pin[n_], v64, v32], writes=[X[n_]])
                p.dve("tensor_copy", out=pin[n_][:m, 0:1], in_=pin[n_][:m, S:S + 1], reads=[pin[n_]], writes=[pin[n_]])
            if stop_after < 3:
                continue
            p.act("activation", out=X["w"][:32, :S], in_=X["w"][:32, :S], func=AF.Tanh, reads=[X["w"]], writes=[X["w"]])
            p.act("activation", out=X["g"][:, :S], in_=X["g"][:, :S], func=AF.Sigmoid, reads=[X["g"]], writes=[X["g"]])
            p.dve("tensor_scalar", out=kk[:, :S], in0=X["k"][:, :S], scalar1=col(v64, KK_), scalar2=None, op0=ALU.mult, reads=[X["k"], v64], writes=[kk])
            p.pool("tensor_tensor", out=kf[:, :S], in0=kk[:, :S], in1=kk[:, :S], op=ALU.mult, reads=[kk], writes=[kf])
            for (c0, n) in tiles_of(S, TW):
                p.pe("matmul", PS[0][:64, :n], w2[:, :], X["w"][:32, c0:c0 + n], start=True, stop=True, reads=[w2, X["w"]], writes=[PS[0]])
                p.act("activation", out=sgw[:, c0:c0 + n], in_=PS[0][:64, :n], func=AF.Sigmoid, bias=col(v64, W0), reads=[PS[0], v64], writes=[sgw])
                p.pe("matmul", PS[1][:64, :n], a2[:, :], X["a"][:32, c0:c0 + n], start=True, stop=True, reads=[a2, X["a"]], writes=[PS[1]])
                p.act("activation", out=aa[:, c0:c0 + n], in_=PS[1][:64, :n], func=AF.Sigmoid, bias=col(v64, A0), reads=[PS[1], v64], writes=[aa])
                p.pe("matmul", PS[2][:64, :n], ones[0:64, 0:64], kf[:, c0:c0 + n], start=True, stop=True, reads=[ones, kf], writes=[PS[2]])
                p.dve("tensor_scalar", out=inv[:, c0:c0 + n], in0=PS[2][:64, :n], scalar1=1e-24, scalar2=None, op0=ALU.max, reads=[PS[2]], writes=[inv])
            p.act("activation", out=inv[:, :S], in_=inv[:, :S], func=AF.Sqrt, reads=[inv], writes=[inv])
            p.dve("reciprocal", out=inv[:, :S], in_=inv[:, :S], reads=[inv], writes=[inv])
            p.dve("tensor_tensor", out=kk[:, :S], in0=kk[:, :S], in1=inv[:, :S], op=ALU.mult, reads=[kk, inv], writes=[kk])
            p.dve("tensor_scalar", out=kf[:, :S], in0=aa[:, :S], scalar1=-1.0, scalar2=col(v64, KA_), op0=ALU.add, op1=ALU.mult, reads=[aa, v64], writes=[kf])
            p.dve("scalar_tensor_tensor", out=kf[:, :S], in0=kf[:, :S], scalar=1.0, in1=X["k"][:, :S], op0=ALU.add, op1=ALU.mult, reads=[kf, X["k"]], writes=[kf])
            p.dve("tensor_scalar", out=sgw[:, :S], in0=sgw[:, :S], scalar1=DEC, scalar2=None, op0=ALU.mult, reads=[sgw], writes=[sgw])
            p.dve("tensor_tensor_scan", out=LW[:, :S], data0=maskS[:, :S], data1=sgw[:, :S], initial=0.0, op0=ALU.mult, op1=ALU.add,
                  reads=[maskS, sgw], writes=[LW])
            p.act("activation", out=Wt[:, :S], in_=LW[:, :S], func=AF.Exp, reads=[LW], writes=[Wt])
            p.act("activation", out=Wti[:, :S], in_=LW[:, :S], func=AF.Exp, scale=-1.0, reads=[LW], writes=[Wti])
            p.pool("tensor_tensor", out=Wex[:, :S], in0=LW[:, :S], in1=sgw[:, :S], op=ALU.subtract, reads=[LW, sgw], writes=[Wex])
            p.act("activation", out=Wex[:, :S], in_=Wex[:, :S], func=AF.Exp, reads=[Wex], writes=[Wex])
            p.dve("scalar_tensor_tensor", out=AR[:, :nch, 0, :], in0=v3(kk)[:, :nch, :], scalar=-1.0, in1=v3(Wex)[:, :nch, :], op0=ALU.mult, op1=ALU.mult,
                  reads=[kk, Wex], writes=[AR])
            p.pool("tensor_tensor", out=AR[:, :nch, 1, :], in0=v3(X["r"])[:, :nch, :], in1=v3(Wt)[:, :nch, :], op=ALU.mult, reads=[X["r"], Wt, AR], writes=[AR])
            p.pool("tensor_tensor", out=Bt[:, :S], in0=kk[:, :S], in1=aa[:, :S], op=ALU.mult, reads=[kk, aa], writes=[Bt])
            p.dve("tensor_tensor", out=Bt[:, :S], in0=Bt[:, :S], in1=Wti[:, :S], op=ALU.mult, reads=[Bt, Wti], writes=[Bt])
            p.pool("tensor_tensor", out=Kt[:, :S], in0=kf[:, :S], in1=Wti[:, :S], op=ALU.mult, reads=[kf, Wti], writes=[Kt])
            p.dve("scalar_tensor_tensor", out=prod[:, :S], in0=X["r"][:, :S], scalar=col(v64, RK_), in1=kf[:, :S], op0=ALU.mult, op1=ALU.mult,
                  reads=[X["r"], kf, v64], writes=[prod])
            for ci in range(nch):
                cs = slice(ci * C, ci * C + C)
                wc = Wt[:, ci * C + C - 1:ci * C + C]
                p.dve("tensor_scalar", out=BWt[:, cs], in0=Bt[:, cs], scalar1=wc, scalar2=None, op0=ALU.mult, reads=[Bt, Wt], writes=[BWt])
                p.pool("tensor_scalar", out=KWt[:, cs], in0=Kt[:, cs], scalar1=wc, scalar2=None, op0=ALU.mult, reads=[Kt, Wt], writes=[KWt])
            if stop_after < 4:
                continue
            for g0 in range(0, nch, G):
                ng = min(G, nch - g0)
                pb = grp % 2
                grp += 1
                tokb, Zb, CMb, Sb, PQb = TOK[pb], Z[pb], CMs[pb], Ss[pb], PQ[pb]
                psT = PS[0:2]
                for gi in range(ng):
                    ci = g0 + gi
                    cs = slice(ci * C, ci * C + C)
                    ps = psT[gi // 2]
                    o = (gi % 2) * 256
                    p.pe("transpose", ps[:64, o:o + 64], AR[:, ci, 0, :], I64, reads=[AR, ident], writes=[ps])
                    p.pe("transpose", ps[:64, o + 64:o + 128], BWt[:, cs], I64, reads=[BWt, ident], writes=[ps])
                    p.pe("transpose", ps[:64, o + 128:o + 192], KWt[:, cs], I64, reads=[KWt, ident], writes=[ps])
                    p.pe("transpose", ps[:64, o + 192:o + 256], X["v"][:, cs], I64, reads=[X["v"], ident], writes=[ps])
                for gi in range(ng):
                    ps = psT[gi // 2]
                    o = (gi % 2) * 256
                    p.act("activation", out=Zb[:, gi, 0:64], in_=ps[:64, o:o + 64], func=AF.Copy, reads=[ps], writes=[Zb])
                    p.dve("tensor_copy", out=tokb[:, gi, :, :].rearrange("p k t -> p (k t)"), in_=ps[:64, o + 64:o + 256], reads=[ps], writes=[tokb])
                for gi in range(ng):
                    ci = g0 + gi
                    cs = slice(ci * C, ci * C + C)
                    ps = PS[2 + gi]
                    arc = AR[:, ci, :, :].rearrange("p k t -> p (k t)")
                    p.pe("matmul", ps[:64, 0:128], Bt[:, cs], arc, start=True, stop=True, reads=[Bt, AR], writes=[ps])
                    p.pe("matmul", ps[:64, 128:256], Kt[:, cs], arc, start=True, stop=True, reads=[Kt, AR], writes=[ps])
                    p.pe("matmul", ps[:64, 256:320], AR[:, ci, 0, :], Bt[:, cs], start=True, stop=True, reads=[Bt, AR], writes=[ps])
                    p.dve("tensor_tensor", out=CMb[:, gi, :], in0=ps[:64, 0:320], in1=cmk[:, :], op=ALU.mult, reads=[ps, cmk], writes=[CMb])
                    p.pool("tensor_tensor", out=Sb[:, gi, :], in0=CMb[:, gi, 0:64], in1=I64, op=ALU.add, reads=[CMb, ident], writes=[Sb])
                if stop_after < 4.3:
                    continue
                psXN, psS = PS[6], PS[7]
                for k in range(1, 6):
                    xb = XN[k % 2]
                    for gi in range(ng):
                        if k == 1:
                            Xp, Np = CMb[:, gi, 0:64], CMb[:, gi, 256:320]
                            rd = [CMb]
                        else:
                            Xp, Np = XN[(k - 1) % 2][:, gi, 0:64], XN[(k - 1) % 2][:, gi, 64:128]
                            rd = [XN[(k - 1) % 2]]
                        if k < 5:
                            p.pe("matmul", psXN[:64, gi * 128:gi * 128 + 64], Np, Xp, start=True, stop=True, reads=rd, writes=[psXN])
                        p.pe("matmul", psXN[:64, gi * 128 + 64:gi * 128 + 128], Xp, Np, start=True, stop=True, reads=rd, writes=[psXN])
                    p.act("activation", out=xb[:, :ng, :].rearrange("p g t -> p (g t)"), in_=psXN[:64, 0:ng * 128], func=AF.Copy, reads=[psXN], writes=[xb])
                    for gi in range(ng):
                        p.pe("matmul", psS[:64, gi * 64:gi * 64 + 64], xb[:, gi, 64:128], Sb[:, gi, :], start=True, stop=True, reads=[xb, Sb], writes=[psS])
                    p.dve("tensor_tensor", out=Sb[:, :ng, :].rearrange("p g t -> p (g t)"), in0=Sb[:, :ng, :].rearrange("p g t -> p (g t)"),
                          in1=psS[:64, 0:ng * 64], op=ALU.add, reads=[Sb, psS], writes=[Sb])
                if stop_after < 4.6:
                    continue
                psA = PS[6]
                for gi in range(ng):
                    p.pe("matmul", psA[:64, gi * 64:gi * 64 + 64], CMb[:, gi, 128:192], tokb[:, gi, 2, :], start=True, stop=True, reads=[CMb, tokb], writes=[psA])
                for gi in range(ng):
                    p.act("activation", out=Zb[:, gi, 64:128], in_=psA[:64, gi * 64:gi * 64 + 64], func=AF.Copy, reads=[psA], writes=[Zb])
                psPQ = PS[7]
                for gi in range(ng):
                    p.pe("matmul", psPQ[:64, gi * 128:gi * 128 + 128], Sb[:, gi, :], Zb[:, gi, :], start=True, stop=True, reads=[Sb, Zb], writes=[psPQ])
                p.dve("tensor_copy", out=PQb[:, :ng, :].rearrange("p g t -> p (g t)"), in_=psPQ[:64, 0:ng * 128], reads=[psPQ], writes=[PQb])
                for gi in range(ng):
                    ci = g0 + gi
                    ps = PS[2 + gi]
                    P_, Q_ = PQb[:, gi, 0:64], PQb[:, gi, 64:128]
                    BW_, KW_, V_ = tokb[:, gi, 0, :], tokb[:, gi, 1, :], tokb[:, gi, 2, :]
                    p.pe("matmul", ps[:64, 0:64], P_, BW_, start=True, stop=True, reads=[PQb, tokb], writes=[ps])
                    p.pe("matmul", ps[:64, 64:128], BW_, Q_, start=True, stop=False, reads=[PQb, tokb], writes=[ps])
                    p.pe("matmul", ps[:64, 64:128], KW_, V_, start=False, stop=True, reads=[tokb], writes=[ps])
                    p.pe("matmul", ps[:64, 128:192], P_, CMb[:, gi, 64:128], start=True, stop=True, reads=[PQb, CMb], writes=[ps])
                    p.pe("matmul", ps[:64, 192:256], CMb[:, gi, 64:128], Q_, start=True, stop=False, reads=[PQb, CMb], writes=[ps])
                    p.pe("matmul", ps[:64, 192:256], CMb[:, gi, 192:256], V_, start=False, stop=True, reads=[CMb, tokb], writes=[ps])
                    wc = Wt[:, ci * C + C - 1:ci * C + C]
                    p.dve("scalar_tensor_tensor", out=GTs[:, ci, :], in0=I64, scalar=wc, in1=ps[:64, 0:64], op0=ALU.mult, op1=ALU.add,
                          reads=[ident, Wt, ps], writes=[(GTs, ci)])
                    p.act("activation", out=Hs[:, ci, :], in_=ps[:64, 64:128], func=AF.Copy, reads=[ps], writes=[(Hs, ci)])
                    p.dve("tensor_tensor", out=RpTs[:, ci, :], in0=ps[:64, 128:192], in1=AR[:, ci, 1, :], op=ALU.add, reads=[ps, AR], writes=[(RpTs, ci)])
                    p.act("activation", out=Yqs[:, ci, :], in_=ps[:64, 192:256], func=AF.Copy, reads=[ps], writes=[(Yqs, ci)])
            if stop_after < 5:
                continue
            if seg > 0:
                p.dve("tensor_copy", out=Mall[:, 0, :], in_=Mall[:, prev_nch, :], reads=[Mall], writes=[Mall])
            for ci in range(nch):
                ps = PS[ci % 2]
                p.pe("matmul", ps[:64, 0:64], GTs[:, ci, :], Mall[:, ci, :], start=True, stop=True, reads=[(GTs, ci), Mall], writes=[ps])
                p.dve("tensor_tensor", out=Mall[:, ci + 1, :], in0=ps[:64, 0:64], in1=Hs[:, ci, :], op=ALU.add, reads=[ps, (Hs, ci)], writes=[Mall])
            prev_nch = nch
            if stop_after < 6:
                continue
            for c8 in range(0, nch, 8):
                n8 = min(8, nch - c8)
                ps = PS[2 + (c8 // 8) % 2]
                for j in range(n8):
                    ci = c8 + j
                    p.pe("matmul", ps[:64, j * 64:j * 64 + 64], RpTs[:, ci, :], Mall[:, ci, :], start=True, stop=True, reads=[(RpTs, ci), Mall], writes=[ps])
                p.dve("tensor_tensor", out=ytok[:, c8:c8 + n8, :].rearrange("p c t -> p (c t)"), in0=ps[:64, 0:n8 * 64],
                      in1=Yqs[:, c8:c8 + n8, :].rearrange("p c t -> p (c t)"), op=ALU.add, reads=[ps, Yqs], writes=[ytok])
            if stop_after < 7:
                continue
            MS, QS, RS, NM, RKR, TMP = range(6)
            p.dve("tensor_reduce", out=st[:, MS, :nch], in_=ytok[:, :nch, :], axis=AX.X, op=ALU.add, reads=[ytok], writes=[st])
            p.pool("tensor_tensor", out=ysq[:, :nch, :], in0=ytok[:, :nch, :], in1=ytok[:, :nch, :], op=ALU.mult, reads=[ytok], writes=[ysq])
            p.dve("tensor_reduce", out=st[:, QS, :nch], in_=ysq[:, :nch, :], axis=AX.X, op=ALU.add, reads=[ysq, st], writes=[st])
            p.dve("tensor_scalar", out=st[:, MS, :nch], in0=st[:, MS, :nch], scalar1=1.0 / 64, scalar2=None, op0=ALU.mult, reads=[st], writes=[st])
            p.dve("tensor_tensor", out=st[:, TMP, :nch], in0=st[:, MS, :nch], in1=st[:, MS, :nch], op=ALU.mult, reads=[st], writes=[st])
            p.dve("scalar_tensor_tensor", out=st[:, RS, :nch], in0=st[:, QS, :nch], scalar=1.0 / 64, in1=st[:, TMP, :nch], op0=ALU.mult, op1=ALU.subtract,
                  reads=[st], writes=[st])
            p.dve("tensor_scalar", out=st[:, RS, :nch], in0=st[:, RS, :nch], scalar1=64e-5, scalar2=None, op0=ALU.add, reads=[st], writes=[st])
            p.act("activation", out=st[:, RS, :nch], in_=st[:, RS, :nch], func=AF.Sqrt, reads=[st], writes=[st])
            p.dve("reciprocal", out=st[:, RS, :nch], in_=st[:, RS, :nch], reads=[st], writes=[st])
            p.dve("scalar_tensor_tensor", out=st[:, NM, :nch], in0=st[:, MS, :nch], scalar=-1.0, in1=st[:, RS, :nch], op0=ALU.mult, op1=ALU.mult,
                  reads=[st], writes=[st])
            psR = PS[4]
            for ci in range(nch):
                p.pe("matmul", psR[:64, ci:ci + 1], prod[:, ci * C:ci * C + C], ones[0:64, 0:1], start=True, stop=True, reads=[prod, ones], writes=[psR])
            p.dve("tensor_copy", out=st[:, RKR, :nch], in_=psR[:64, 0:nch], reads=[psR, st], writes=[st])
            for c8 in range(0, nch, 8):
                n8 = min(8, nch - c8)
                psG = PS[5]
                psO = PS[6 + (c8 // 8) % 2]
                for j in range(n8):
                    ci = c8 + j
                    p.pe("matmul", psG[:64, j * 64:j * 64 + 64], X["g"][:, ci * C:ci * C + C], g2[:, :], start=True, stop=True, reads=[X["g"], g2], writes=[psG])
                for j in range(n8):
                    ci = c8 + j
                    p.act("activation", out=t1[:, :], in_=ytok[:, ci, :], func=AF.Identity, scale=st[:, RS, ci:ci + 1], bias=st[:, NM, ci:ci + 1],
                          reads=[ytok, st], writes=[t1])
                    p.dve("tensor_tensor", out=t1[:, :], in0=t1[:, :], in1=gn[:, 0:64], op=ALU.mult, reads=[t1, gn], writes=[t1])
                    p.pool("tensor_tensor", out=t2[:, :], in0=t1[:, :], in1=gn[:, 64:128], op=ALU.add, reads=[t1, gn], writes=[t2])
                    pbv = None
                    p.pe("transpose", psO[:64, j * 64:j * 64 + 64], X["v"][:, ci * C:ci * C + C], I64, reads=[X["v"], ident], writes=[psO])
                    p.dve("scalar_tensor_tensor", out=t2[:, :], in0=psO[:64, j * 64:j * 64 + 64], scalar=st[:, RKR, ci:ci + 1], in1=t2[:, :], op0=ALU.mult, op1=ALU.add,
                          reads=[psO, st, t2], writes=[t2])
                    p.dve("tensor_tensor", out=otok[:, ci, :], in0=t2[:, :], in1=psG[:64, j * 64:j * 64 + 64], op=ALU.mult, reads=[t2, psG], writes=[otok])
                for j in range(n8):
                    ci = c8 + j
                    p.pe("transpose", psO[:64, j * 64:j * 64 + 64], otok[:, ci, :], I64, reads=[otok, ident], writes=[psO])
                p.act("activation", out=obuf[:, c8 * C:(c8 + n8) * C], in_=psO[:64, 0:n8 * 64], func=AF.Copy, reads=[psO], writes=[obuf])
            p.dma("sync", out_d[:, tok0:tok0 + real], obuf[:, :real], reads=[obuf], writes=[(out_d, seg)])


def build_p1(T, which=("lru", "fox", "mla", "rwkv"), TW=512, stop_after=99, SEG=2052):
    p = Prog()
    hb_d = p.dram_in("hb", [128, KC, T], BF16)
    c = make_consts(p)
    PS = [p.psum("ps%d" % i, [128, 512]) for i in range(8)]
    if "lru" in which:
        W = {"wy": p.dram_in("lru_wy", [128, KC, 64], F32), "wx": p.dram_in("lru_wx", [128, KC, 64], F32),
             "wrg": p.dram_in("lru_wrg", [64, 64], F32), "wig": p.dram_in("lru_wig", [64, 64], F32),
             "lv": p.dram_in("lru_lv", [64, 8], F32)}
        out_a = p.dram_out("o_lru", [64, T], BF16)
        emit_lru(p, c, PS, hb_d, T, W, out_a, TW, SEG=SEG)
    if "fox" in which:
        W = {"wq": p.dram_in("fox_wq", [128, KC, 128], F32), "wk": p.dram_in("fox_wk", [128, KC, 128], F32),
             "wv": p.dram_in("fox_wv", [128, KC, 64], F32), "fv": p.dram_in("fox_fv", [1, 2], F32)}
        out_b = p.dram_out("o_fox", [64, T], BF16)
        emit_fox(p, c, PS, hb_d, T, W, out_b, TW, stop_after=stop_after)
    if "mla" in which:
        W = {"wcq": p.dram_in("mla_wcq", [128, KC, 256], F32), "wckv": p.dram_in("mla_wckv", [128, KC, 128], F32),
             "wkr": p.dram_in("mla_wkr", [128, KC, 64], F32), "wkrs": p.dram_in("mla_wkrs", [128, KC, 64], F32),
             "gv": p.dram_in("mla_gv", [128, 3], F32), "wuq": p.dram_in("mla_wuq", [128, 2, 128], F32),
             "wuqs": p.dram_in("mla_wuqs", [128, 2, 128], F32), "wukv": p.dram_in("mla_wukv", [128, 192], F32),
             "cos2": p.dram_in("mla_cos2", [32, T], F32), "sin2": p.dram_in("mla_sin2", [32, T], F32)}
        out_c = p.dram_out("o_mla", [64, T], BF16)
        emit_mla(p, c, PS, hb_d, T, W, out_c, TW, stop_after=stop_after)
    if "rwkv" in which:
        W = {"wrkv": p.dram_in("rwkv_wrkv", [128, KC, 192], F32), "wlow": p.dram_in("rwkv_wlow", [128, KC, 128], F32),
             "v64": p.dram_in("rwkv_v64", [64, 12], F32), "v32": p.dram_in("rwkv_v32", [32, 2], F32),
             "w2": p.dram_in("rwkv_w2", [32, 64], F32), "a2": p.dram_in("rwkv_a2", [32, 64], F32),
             "g2": p.dram_in("rwkv_g2", [64, 64], F32), "gn": p.dram_in("rwkv_gn", [64, 128], F32)}
        out_dd = p.dram_out("o_rwkv", [64, T], BF16)
        emit_rwkv(p, c, PS, hb_d, T, W, out_dd, TW, stop_after=stop_after)
    p.finish()
    return p


def rope_tables(T):
    inv = (1.0 / (np.float32(10000.0) ** (np.arange(0, 32, 2, dtype=np.float32) / np.float32(32)))).astype(np.float32)
    ang = (np.arange(T, dtype=np.float32)[:, None] * inv[None, :]).astype(np.float32)
    cos, sin = np.cos(ang).astype(np.float32), np.sin(ang).astype(np.float32)
    cos2 = np.ascontiguousarray(np.concatenate([cos, cos], 1).T)
    sin2 = np.ascontiguousarray(np.concatenate([-sin, sin], 1).T)
    return cos2, sin2


def p1_maps(P, l, hT_b, NCORE):
    B = hT_b.shape[0]
    T = hT_b.shape[3]
    w_in = P["w_in"][l]
    cos2, sin2 = rope_tables(T)
    maps = []
    for cidx in range(NCORE):
        b, hd = divmod(cidx, 4)
        b = min(b, B - 1)
        m = {"hb": hT_b[b]}
        hs = slice(hd * 64, hd * 64 + 64)
        m["lru_wy"] = fm_w(w_in[:, 0 + hd * 64:0 + hd * 64 + 64])
        m["lru_wx"] = fm_w(w_in[:, 256 + hd * 64:256 + hd * 64 + 64])
        m["lru_wrg"] = np.ascontiguousarray(P["lru_w_rg"][l][hd])
        m["lru_wig"] = np.ascontiguousarray(P["lru_w_ig"][l][hd])
        m["lru_lv"] = np.ascontiguousarray(np.stack([P["lru_conv_w"][l][0, hs], P["lru_conv_w"][l][1, hs], P["lru_conv_w"][l][2, hs],
                                                     P["lru_conv_w"][l][3, hs], P["lru_conv_b"][l][hs], P["lru_b_rg"][l][hs],
                                                     P["lru_b_ig"][l][hs], P["lru_lambda"][l][hs]], 1))
        wq = np.zeros((1024, 128), np.float32)
        wq[:, 64:128] = w_in[:, 512 + hd * 64:512 + hd * 64 + 64]
        wq[:, 0] = w_in[:, 1280 + hd]
        m["fox_wq"] = fm_w(wq)
        wk = np.zeros((1024, 128), np.float32)
        wk[:, 64:128] = w_in[:, 768 + hd * 64:768 + hd * 64 + 64]
        m["fox_wk"] = fm_w(wk)
        m["fox_wv"] = fm_w(w_in[:, 1024 + hd * 64:1024 + hd * 64 + 64])
        fv = np.zeros((1, 2), np.float32)
        fv[0, 0] = P["fox_b_f"][l][hd]
        m["fox_fv"] = fv
        m["mla_wcq"] = fm_w(w_in[:, 1284:1540])
        m["mla_wckv"] = fm_w(w_in[:, 1540:1668])
        kr = w_in[:, 1668:1700]
        wkr = np.zeros((1024, 64), np.float32)
        wkr[:, 32:64] = kr
        wkrs = np.zeros((1024, 64), np.float32)
        wkrs[:, 32:48] = kr[:, 16:32]
        wkrs[:, 48:64] = kr[:, 0:16]
        m["mla_wkr"] = fm_w(wkr)
        m["mla_wkrs"] = fm_w(wkrs)
        gv = np.zeros((128, 3), np.float32)
        gv[:, 0:2] = vec_fm(P["mla_q_norm_g"][l])
        gv[:, 2] = P["mla_kv_norm_g"][l]
        m["mla_gv"] = gv
        uq = P["mla_w_uq"][l][:, hd * 96:(hd + 1) * 96]
        uqa = np.zeros((256, 128), np.float32)
        uqa[:, 64:128] = uq[:, 0:64]
        uqa[:, 32:64] = uq[:, 64:96]
        m["mla_wuq"] = fm_w(uqa)
        uqs = np.zeros((256, 128), np.float32)
        uqs[:, 32:48] = uq[:, 80:96]
        uqs[:, 48:64] = uq[:, 64:80]
        m["mla_wuqs"] = fm_w(uqs)
        ukv = P["mla_w_ukv"][l][:, hd * 128:(hd + 1) * 128]
        ukva = np.zeros((128, 192), np.float32)
        ukva[:, 64:128] = ukv[:, 0:64]
        ukva[:, 128:192] = ukv[:, 64:128]
        m["mla_wukv"] = ukva
        m["mla_cos2"] = cos2
        m["mla_sin2"] = sin2
        R0 = 1700
        m["rwkv_wrkv"] = fm_w(np.concatenate([w_in[:, R0 + hd * 64:R0 + hd * 64 + 64], w_in[:, R0 + 256 + hd * 64:R0 + 256 + hd * 64 + 64],
                                              w_in[:, R0 + 512 + hd * 64:R0 + 512 + hd * 64 + 64]], 1))
        m["rwkv_wlow"] = fm_w(w_in[:, R0 + 768:R0 + 896])
        mu = P["rwkv_mu"][l]
        v64 = np.zeros((64, 12), np.float32)
        v64[:, 0] = mu[0 + hd * 64:0 + hd * 64 + 64]
        v64[:, 1] = mu[256 + hd * 64:256 + hd * 64 + 64]
        v64[:, 2] = mu[512 + hd * 64:512 + hd * 64 + 64]
        v64[:, 3] = mu[832:896]
        v64[:, 4] = P["rwkv_w0"][l][hs]
        v64[:, 5] = P["rwkv_a0"][l][hs]
        v64[:, 6] = P["rwkv_k_k"][l][hs]
        v64[:, 7] = P["rwkv_k_a"][l][hs]
        v64[:, 8] = P["rwkv_r_k"][l][hd]
        m["rwkv_v64"] = v64
        m["rwkv_v32"] = np.ascontiguousarray(np.stack([mu[768:800], mu[800:832]], 1))
        m["rwkv_w2"] = np.ascontiguousarray(P["rwkv_w2"][l][:, hs])
        m["rwkv_a2"] = np.ascontiguousarray(P["rwkv_a2"][l][:, hs])
        m["rwkv_g2"] = np.ascontiguousarray(P["rwkv_g2"][l][:, hs])
        m["rwkv_gn"] = np.ascontiguousarray(np.concatenate([np.broadcast_to(P["rwkv_gn_g"][l][hs][None], (64, 64)),
                                                            np.broadcast_to(P["rwkv_gn_b"][l][hs][None], (64, 64))], 1))
        maps.append(m)
    return maps

def fm(a):
    T, F = a.shape
    return np.ascontiguousarray(a.T.reshape(F // 128, 128, T).transpose(1, 0, 2))


def unfm(a):
    P, kc, T = a.shape
    return np.ascontiguousarray(a.transpose(1, 0, 2).reshape(kc * P, T).T)


def fm_w(w):
    K_, M = w.shape
    return np.ascontiguousarray(w.reshape(K_ // 128, 128, M).transpose(1, 0, 2))


def vec_fm(v):
    return np.ascontiguousarray(v.reshape(-1, 128).T)


def p2_maps(P, l, hT_f, hT_b, oT_b, NC):
    B, _, _, T = hT_f.shape
    per_b = NC // B
    NOUT = T // per_b
    w_in = P["w_in"][l]
    wg = fm_w(w_in[:, 2596:6692])
    wb = fm_w(P["w_branch"][l].reshape(1024, 1024))
    wo = fm_w(P["w_out"][l])
    wu = P["ffn_w_up"][l].reshape(8, 128, 2 * DFF)
    wup = np.stack([np.concatenate([wu[:, :, fc * 128:(fc + 1) * 128], wu[:, :, DFF + fc * 128:DFF + (fc + 1) * 128]], 2)
                    .transpose(1, 0, 2).reshape(128, KC * 256) for fc in range(NFC)], 0)
    wup = np.ascontiguousarray(wup)
    wdn = np.ascontiguousarray(P["ffn_w_down"][l].reshape(NFC, 128, 1024))
    vecs = np.zeros((128, NV), np.float32)
    vecs[:, V_LNMIX_G:V_LNMIX_G + 8] = vec_fm(P["ln_mix_g"][l])
    vecs[:, V_LNMIX_B:V_LNMIX_B + 8] = vec_fm(P["ln_mix_b"][l])
    vecs[:, V_LNFFN_G:V_LNFFN_G + 8] = vec_fm(P["ln_ffn_g"][l])
    vecs[:, V_LNFFN_B:V_LNFFN_B + 8] = vec_fm(P["ln_ffn_b"][l])
    vecs[:, V_CW:V_CW + 132] = P["ffn_conv_w"][l].reshape(3, 44, 128).transpose(2, 0, 1).reshape(128, 132)
    vecs[:, V_CB:V_CB + 44] = P["ffn_conv_b"][l].reshape(44, 128).T
    maps = []
    for c in range(NC):
        b, j = divmod(c, per_b)
        t0 = j * NOUT
        lo = max(t0 - 2, 0)
        pad = 2 - (t0 - lo)

        def sl(a):
            out = np.zeros((128, KC, NOUT + 2), a.dtype)
            out[:, :, pad:] = a[b][:, :, lo:t0 + NOUT]
            return out
        v = vecs.copy()
        v[:, V_FLAG] = 0.0 if j == 0 else 1.0
        maps.append({"hf": sl(hT_f), "hb": sl(hT_b), "ob": sl(oT_b), "wg": wg, "wb": wb, "wo": wo,
                     "vecs": v, "wup": wup, "wdn": wdn})
    return maps


_PROGS = {}


def _prog(name, fn):
    if name not in _PROGS:
        _PROGS[name] = fn()
    return _PROGS[name]


def kernel(**inputs):
    P = {k: np.asarray(v) for k, v in inputs.items()}
    x = P["x"]
    B, SEQ, _ = x.shape
    T = SEQ + 16
    NC = 8
    per_b = NC // B
    NOUT = T // per_b
    cores = list(range(NC))
    p0 = build_p0(NOUT)
    vec0 = np.concatenate([vec_fm(P["ln_in_g"]), vec_fm(P["ln_in_b"])], 1)
    maps = []
    for c in range(NC):
        b, j = divmod(c, per_b)
        xcat = np.concatenate([P["meta_tokens"], x[b]], 0)[j * NOUT:(j + 1) * NOUT]
        maps.append({"x": fm(xcat), "vecs": vec0})
    res = run_bass_kernel_spmd(p0.nc, maps, core_ids=cores).results
    hT_f = np.stack([np.concatenate([res[b * per_b + j]["of"] for j in range(per_b)], 2) for b in range(B)], 0)
    hT_b = np.stack([np.concatenate([res[b * per_b + j]["ob"] for j in range(per_b)], 2) for b in range(B)], 0)
    p1 = build_p1(T)
    p2 = build_p2(NOUT)
    for l in range(2):
        res = run_bass_kernel_spmd(p1.nc, p1_maps(P, l, hT_b, NC), core_ids=cores).results
        oT_b = np.zeros((B, 128, KC, T), hT_b.dtype)
        for c in range(NC):
            b, hd = divmod(c, 4)
            for n, nm in enumerate(("o_lru", "o_fox", "o_mla", "o_rwkv")):
                ch = n * 256 + hd * 64
                kc, p0_ = divmod(ch, 128)
                oT_b[b, p0_:p0_ + 64, kc, :] = res[c][nm]
        res = run_bass_kernel_spmd(p2.nc, p2_maps(P, l, hT_f, hT_b, oT_b, NC), core_ids=cores).results
        hT_f = np.stack([np.concatenate([res[b * per_b + j]["of"] for j in range(per_b)], 2) for b in range(B)], 0)
        hT_b = np.stack([np.concatenate([res[b * per_b + j]["obf"] for j in range(per_b)], 2) for b in range(B)], 0)
    out = np.stack([unfm(hT_f[b])[16:] for b in range(B)], 0)
    return np.ascontiguousarray(out.astype(np.float32))
```
